# Optimizing a Trainium2 kernel written in Bass

```python
import math
import jax, jax.numpy as jnp
from jax import lax
import numpy as np

D_MODEL = 1024
BATCH = 1
SEQ = 16384
DEPTH = 4
DEC_BATCH = 32
DEC_SEQ = 64
PAST_LEN = 2048

CHUNK = 64
D_MIX = D_MODEL
D_B = D_MIX // 4
POOL_WINDOWS = (2, 4, 8, 16)
N_POOL = len(POOL_WINDOWS)
POOL_GROUP = D_B // N_POOL
POOL_BUF = max(POOL_WINDOWS) - 1
HEAD_DIM = 64
N_HEADS_C = 4
D_C = N_HEADS_C * HEAD_DIM
Q_BLOCK = 128
K_BLOCK = 128
D_A = D_MIX - D_B - D_C
LRU_BLOCK = 64
N_LRU_BLOCKS = D_A // LRU_BLOCK
CONV_W = 4
LRU_C = 8.0
D_IN = 2 * D_A + D_B + 3 * D_C
SPLITS = (D_A, 2 * D_A, 2 * D_A + D_B, 2 * D_A + D_B + D_C, 2 * D_A + D_B + 2 * D_C)
D_FF = 256 * (-(-(8 * D_MODEL) // (3 * 256)))
EPS = 1e-6

kernel_name = "hymba_rglru_pool_stickbreaking_stream_step"


def rms_norm(x, g):
    xf = x.astype(jnp.float32)
    y = xf * lax.rsqrt(jnp.mean(xf * xf, axis=-1, keepdims=True) + EPS)
    return (y * g.astype(jnp.float32)).astype(x.dtype)


def causal_conv(x, buf, w, b):
    L = x.shape[1]
    xp = jnp.concatenate([buf, x], axis=1)
    y = b + sum(xp[:, j:j + L] * w[j] for j in range(CONV_W))
    return y, xp[:, -(CONV_W - 1):]


def rg_lru(x, h0, ga_w, ga_b, gx_w, gx_b, lam):
    B, L, _ = x.shape
    xb = x.reshape(B, L, N_LRU_BLOCKS, LRU_BLOCK)
    r = jax.nn.sigmoid(jnp.einsum("blhi,hij->blhj", xb, ga_w) + ga_b).reshape(B, L, D_A)
    i = jax.nn.sigmoid(jnp.einsum("blhi,hij->blhj", xb, gx_w) + gx_b).reshape(B, L, D_A)
    log_a = -LRU_C * r.astype(jnp.float32) * jax.nn.softplus(-lam.astype(jnp.float32))
    a = jnp.exp(log_a)
    u = jnp.sqrt(-jnp.expm1(2.0 * log_a)) * (i.astype(jnp.float32) * x.astype(jnp.float32))

    def combine(left, right):
        a1, b1 = left
        a2, b2 = right
        return a1 * a2, a2 * b1 + b2

    a_cum, b_cum = lax.associative_scan(combine, (a, u), axis=1)
    h = a_cum * h0.astype(jnp.float32)[:, None, :] + b_cum
    return h.astype(x.dtype), h[:, -1].astype(x.dtype)


def multiscale_pool(u, buf, pos0, pool_w, pool_scale):
    B, L, _ = u.shape
    up = jnp.concatenate([buf, u], axis=1).astype(jnp.float32)
    pos = pos0 + jnp.arange(L)
    means = []
    for g, w in enumerate(POOL_WINDOWS):
        sl = slice(g * POOL_GROUP, (g + 1) * POOL_GROUP)
        s = sum(up[:, POOL_BUF - j:POOL_BUF - j + L, sl] for j in range(w))
        cnt = jnp.minimum(pos + 1, w).astype(jnp.float32)[None, :, None]
        means.append(s / cnt)
    d = (jnp.concatenate(means, axis=-1) - u.astype(jnp.float32)).reshape(B, L, N_POOL, POOL_GROUP)
    y = jnp.einsum("blgi,gij->blgj", d, pool_w.astype(jnp.float32)).reshape(B, L, D_B)
    y = y * pool_scale.astype(jnp.float32)
    new_buf = jnp.concatenate([buf, u], axis=1)[:, -POOL_BUF:]
    return y.astype(u.dtype), new_buf


def stick_breaking(q, k, v, q_pos, k_pos):
    B, Tq, H, _ = q.shape
    Tk = k.shape[1]
    n_kb = Tk // K_BLOCK
    z = jnp.einsum("bqhd,bkhd->bhqk", q, k, preferred_element_type=jnp.float32)
    mask = k_pos[None, :] < q_pos[:, None]
    nl = jnp.where(mask, jnp.log1p(jnp.exp(z)), 0.0).reshape(B, H, Tq, n_kb, K_BLOCK)
    tri_in = jnp.asarray(np.tril(np.ones((K_BLOCK, K_BLOCK), np.float32)))
    tri_ex = jnp.asarray(np.tril(np.ones((n_kb, n_kb), np.float32), -1))
    within = jnp.einsum("bhqnj,js->bhqns", nl, tri_in)
    later = jnp.einsum("bhqm,mn->bhqn", nl.sum(axis=-1), tri_ex)
    surv = (within + later[..., None]).reshape(B, H, Tq, Tk)
    attn = jnp.where(mask, jnp.exp(z - surv), 0.0)
    return jnp.einsum("bhqk,bkhd->bqhd", attn.astype(v.dtype), v)


def stick_breaking_prompt(q, k, v):
    S = q.shape[1]
    outs = []
    for i in range(S // Q_BLOCK):
        kend = (i + 1) * Q_BLOCK
        q_pos = i * Q_BLOCK + jnp.arange(Q_BLOCK)
        outs.append(stick_breaking(q[:, i * Q_BLOCK:kend], k[:, :kend], v[:, :kend], q_pos, jnp.arange(kend)))
    return jnp.concatenate(outs, axis=1)


def stick_breaking_sample(q, k_past, v_past, k, v, pos0):
    B, L, H, D = q.shape
    tk = k_past.shape[1] + L
    pad = (-tk) % K_BLOCK
    zpad = jnp.zeros((B, pad, H, D), k.dtype)
    kf = jnp.concatenate([k_past, k, zpad], axis=1)
    vf = jnp.concatenate([v_past, v, zpad], axis=1)
    return stick_breaking(q, kf, vf, pos0 + jnp.arange(L), jnp.arange(tk + pad))


def trunk_layer(x, lw, conv_buf, lru_h, pool_buf, k_past, v_past, pos0):
    B, L, _ = x.shape
    h = rms_norm(x, lw["attn_norm"])
    proj = jnp.einsum("bld,de->ble", h, lw["w_in"])
    x_a, g_a, u_b, q, k, v = jnp.split(proj, SPLITS, axis=-1)
    xc, new_conv = causal_conv(x_a, conv_buf, lw["conv_w"], lw["conv_b"])
    y_lru, new_h = rg_lru(xc, lru_h, lw["gate_a_w"], lw["gate_a_b"], lw["gate_x_w"], lw["gate_x_b"], lw["lru_lambda"])
    y_a = y_lru * jax.nn.gelu(g_a, approximate=True)
    y_b, new_pool = multiscale_pool(u_b, pool_buf, pos0, lw["pool_w"], lw["pool_scale"])
    q = rms_norm(q.reshape(B, L, N_HEADS_C, HEAD_DIM), lw["q_norm"]) * (HEAD_DIM ** -0.5)
    k = rms_norm(k.reshape(B, L, N_HEADS_C, HEAD_DIM), lw["k_norm"])
    v = v.reshape(B, L, N_HEADS_C, HEAD_DIM)
    if k_past is None:
        o = stick_breaking_prompt(q, k, v)
    else:
        o = stick_breaking_sample(q, k_past, v_past, k, v, pos0)
    mixed = jnp.concatenate([y_a, y_b, o.reshape(B, L, D_C)], axis=-1)
    x = x + jnp.einsum("ble,ed->bld", mixed, lw["w_out"])
    h2 = rms_norm(x, lw["ffn_norm"])
    ff = jax.nn.silu(h2 @ lw["w_gate"]) * (h2 @ lw["w_up"])
    x = x + ff @ lw["w_down"]
    return x, new_conv, new_h, new_pool, k, v


def setup_inputs(seed: int = 0) -> dict:
    key = jax.random.key(seed)
    ks = jax.random.split(key, 32)
    f32 = jnp.float32
    nrm = lambda k, shape, s: jax.random.normal(k, shape, f32) * s
    out_scale = (2.0 * DEPTH) ** -0.5
    u = jax.random.uniform(ks[14], (DEPTH, D_A), f32, 0.9, 0.999)
    a = u ** (1.0 / LRU_C)
    lru_lambda = jnp.log(a) - jnp.log1p(-a)
    return {
        "x_prompt": nrm(ks[0], (BATCH, SEQ, D_MODEL), 1.0),
        "x_sample": nrm(ks[1], (DEC_BATCH, DEC_SEQ, D_MODEL), 1.0),
        "cache_k": nrm(ks[2], (DEPTH, DEC_BATCH, PAST_LEN, N_HEADS_C, HEAD_DIM), 1.0),
        "cache_v": nrm(ks[3], (DEPTH, DEC_BATCH, PAST_LEN, N_HEADS_C, HEAD_DIM), 1.0),
        "state_conv": nrm(ks[4], (DEPTH, DEC_BATCH, CONV_W - 1, D_A), 1.0),
        "state_lru": nrm(ks[5], (DEPTH, DEC_BATCH, D_A), 0.5),
        "state_pool": nrm(ks[6], (DEPTH, DEC_BATCH, POOL_BUF, D_B), 1.0),
        "attn_norm": 1.0 + nrm(ks[7], (DEPTH, D_MODEL), 0.02),
        "w_in": nrm(ks[8], (DEPTH, D_MODEL, D_IN), D_MODEL ** -0.5),
        "conv_w": nrm(ks[9], (DEPTH, CONV_W, D_A), CONV_W ** -0.5),
        "conv_b": nrm(ks[10], (DEPTH, D_A), 0.01),
        "gate_a_w": nrm(ks[11], (DEPTH, N_LRU_BLOCKS, LRU_BLOCK, LRU_BLOCK), LRU_BLOCK ** -0.5),
        "gate_a_b": nrm(ks[12], (DEPTH, N_LRU_BLOCKS, LRU_BLOCK), 0.01),
        "gate_x_w": nrm(ks[13], (DEPTH, N_LRU_BLOCKS, LRU_BLOCK, LRU_BLOCK), LRU_BLOCK ** -0.5),
        "gate_x_b": nrm(ks[15], (DEPTH, N_LRU_BLOCKS, LRU_BLOCK), 0.01),
        "lru_lambda": lru_lambda,
        "pool_w": nrm(ks[16], (DEPTH, N_POOL, POOL_GROUP, POOL_GROUP), POOL_GROUP ** -0.5),
        "pool_scale": 1.0 + nrm(ks[17], (DEPTH, D_B), 0.02),
        "q_norm": 1.0 + nrm(ks[18], (DEPTH, HEAD_DIM), 0.02),
        "k_norm": 1.0 + nrm(ks[19], (DEPTH, HEAD_DIM), 0.02),
        "w_out": nrm(ks[20], (DEPTH, D_MIX, D_MODEL), D_MIX ** -0.5 * out_scale),
        "ffn_norm": 1.0 + nrm(ks[21], (DEPTH, D_MODEL), 0.02),
        "w_gate": nrm(ks[22], (DEPTH, D_MODEL, D_FF), D_MODEL ** -0.5),
        "w_up": nrm(ks[23], (DEPTH, D_MODEL, D_FF), D_MODEL ** -0.5),
        "w_down": nrm(ks[24], (DEPTH, D_FF, D_MODEL), D_FF ** -0.5 * out_scale),
    }


def reference(x_prompt, x_sample, cache_k, cache_v, state_conv, state_lru, state_pool,
              attn_norm, w_in, conv_w, conv_b, gate_a_w, gate_a_b, gate_x_w, gate_x_b,
              lru_lambda, pool_w, pool_scale, q_norm, k_norm, w_out, ffn_norm,
              w_gate, w_up, w_down):
    xp = x_prompt
    xs = x_sample
    dt = x_prompt.dtype
    kp_l, vp_l, cp_l, hp_l, pp_l = [], [], [], [], []
    ks_l, vs_l, cs_l, hs_l, ps_l = [], [], [], [], []
    for l in range(DEPTH):
        lw = dict(attn_norm=attn_norm[l], w_in=w_in[l], conv_w=conv_w[l], conv_b=conv_b[l],
                  gate_a_w=gate_a_w[l], gate_a_b=gate_a_b[l], gate_x_w=gate_x_w[l], gate_x_b=gate_x_b[l],
                  lru_lambda=lru_lambda[l], pool_w=pool_w[l], pool_scale=pool_scale[l],
                  q_norm=q_norm[l], k_norm=k_norm[l], w_out=w_out[l], ffn_norm=ffn_norm[l],
                  w_gate=w_gate[l], w_up=w_up[l], w_down=w_down[l])
        xp, c_p, h_p, p_p, k_p, v_p = trunk_layer(
            xp, lw,
            jnp.zeros((xp.shape[0], CONV_W - 1, D_A), dt),
            jnp.zeros((xp.shape[0], D_A), dt),
            jnp.zeros((xp.shape[0], POOL_BUF, D_B), dt),
            None, None, 0)
        xs, c_s, h_s, p_s, k_s, v_s = trunk_layer(
            xs, lw, state_conv[l], state_lru[l], state_pool[l], cache_k[l], cache_v[l], PAST_LEN)
        kp_l.append(k_p); vp_l.append(v_p); cp_l.append(c_p); hp_l.append(h_p); pp_l.append(p_p)
        ks_l.append(k_s); vs_l.append(v_s); cs_l.append(c_s); hs_l.append(h_s); ps_l.append(p_s)
    return (xp, xs,
            jnp.stack(kp_l), jnp.stack(vp_l), jnp.stack(cp_l), jnp.stack(hp_l), jnp.stack(pp_l),
            jnp.stack(ks_l), jnp.stack(vs_l), jnp.stack(cs_l), jnp.stack(hs_l), jnp.stack(ps_l))
```

```python
import numpy as np
import concourse.bass as bass
import concourse.mybir as mybir
from concourse.bass_utils import run_bass_kernel_spmd

F32 = mybir.dt.float32
BF16 = mybir.dt.bfloat16
AF = mybir.ActivationFunctionType
ALU = mybir.AluOpType
AX = mybir.AxisListType

CFG = {"SEQ": 16384, "DEPTH": 4, "PAST": 2048}
D = 1024
DIN = 2048
DFF = 2816
T = 512
NSQ = 4
LS = 64
EPS = 1e-6
NPRM = 34
NDMA = 6


class Sched:
    def __init__(self, nc):
        self.nc = nc
        self.eng = {"pe": nc.tensor, "act": nc.scalar, "dve": nc.vector, "pool": nc.gpsimd, "sp": nc.sync}
        self.sem = {e: nc.alloc_semaphore(f"sem_{e}") for e in self.eng}
        self.cnt = {e: 0 for e in self.eng}
        self.seen = {e: {} for e in self.eng}
        self.last_w = {}
        self.readers = {}
        self.dsem = {q: [[nc.alloc_semaphore(f"dsem_{q}{i}"), 0] for i in range(NDMA)] for q in ("sp", "pool")}
        self.dnext = {"sp": 0, "pool": 0}
        self.ninstr = 0

    def _wait(self, e, deps):
        eng = self.eng[e]
        best = {}
        for h in deps:
            if h is None:
                continue
            key, val = h
            if best.get(key, 0) < val:
                best[key] = val
        for key, val in best.items():
            if self.seen[e].get(key, 0) >= val:
                continue
            sem = self.sem[key] if isinstance(key, str) else self.dsem[key[0]][key[1]][0]
            eng.wait_ge(sem, val)
            self.ninstr += 1
            self.seen[e][key] = val

    def _deps(self, reads, writes):
        deps = []
        for k in reads:
            deps.append(self.last_w.get(k))
        for k in writes:
            deps.append(self.last_w.get(k))
            deps.extend(self.readers.get(k, {}).values())
        return deps

    def _record(self, h, reads, writes):
        for k in writes:
            self.last_w[k] = h
            self.readers[k] = {}
        for k in reads:
            self.readers.setdefault(k, {})[h[0]] = h

    def op(self, e, emit, reads=(), writes=()):
        self._wait(e, self._deps(reads, writes))
        ins = emit(self.eng[e])
        self.cnt[e] += 1
        ins.then_inc(self.sem[e], 1)
        self.ninstr += 1
        h = (e, self.cnt[e])
        self._record(h, reads, writes)
        return h

    def dma(self, q, out, in_, reads=(), writes=(), **kw):
        self._wait(q, self._deps(reads, writes))
        i = self.dnext[q] % NDMA
        self.dnext[q] += 1
        slot = self.dsem[q][i]
        if slot[1] > 0 and self.seen[q].get((q, i), 0) < slot[1]:
            self.eng[q].wait_ge(slot[0], slot[1])
            self.seen[q][(q, i)] = slot[1]
        ins = self.eng[q].dma_start(out=out, in_=in_, **kw)
        slot[1] += 16
        ins.then_inc(slot[0], 16)
        self.ninstr += 1
        h = ((q, i), slot[1])
        self._record(h, reads, writes)
        return h

    def finish(self):
        for q in ("sp", "pool"):
            for i, slot in enumerate(self.dsem[q]):
                if slot[1] > 0:
                    self.eng[q].wait_ge(slot[0], slot[1])


def build_program(cfg):
    SEQ, DEPTH, PAST = cfg["SEQ"], cfg["DEPTH"], cfg["PAST"]
    NCH = SEQ // T
    NPB = PAST // 128
    nc = bass.Bass("TRN2", target_bir_lowering=False)
    S = Sched(nc)

    def din(name, shape, dt=F32):
        return nc.dram_tensor(name, list(shape), dt, kind="ExternalInput").ap()

    def dout(name, shape, dt=F32):
        return nc.dram_tensor(name, list(shape), dt, kind="ExternalOutput").ap()

    xp = din("xp", [SEQ, D]); xs = din("xs", [NSQ * LS, D])
    ckT = din("ckT", [DEPTH, NSQ, 64, 4, PAST]); cv = din("cv", [DEPTH, NSQ, PAST, 256])
    sconv = din("sconv", [DEPTH, 512, NSQ, 3]); slru = din("slru", [DEPTH, 512, NSQ]); spool = din("spool", [DEPTH, 256, NSQ, 15])
    w_in = din("w_in", [DEPTH, D, DIN]); w_out = din("w_out", [DEPTH, D, D])
    w_gate = din("w_gate", [DEPTH, D, DFF]); w_up = din("w_up", [DEPTH, D, DFF]); w_down = din("w_down", [DEPTH, DFF, D])
    gbd = din("gbd", [DEPTH, 2, 4, 128, 128]); pbd = din("pbd", [DEPTH, 2, 128, 128])
    prm = din("prm", [DEPTH, 128, NPRM]); grow = din("grow", [DEPTH, 2, 128, D]); qkg = din("qkg", [DEPTH, 128, 512])
    cstf = din("cstf", [128, 768]); invcnt = din("invcnt", [128, 2, 16])

    yp = dout("yp", [SEQ, D]); ys = dout("ys", [NSQ * LS, D])
    kp = dout("kp", [DEPTH, SEQ, 256]); vp = dout("vp", [DEPTH, SEQ, 256])
    convp = dout("convp", [DEPTH, 512, 3]); lrup = dout("lrup", [DEPTH, 512, 1]); poolp = dout("poolp", [DEPTH, 256, 15])
    ksm = dout("ksm", [DEPTH, NSQ * LS, 256]); vsm = dout("vsm", [DEPTH, NSQ * LS, 256])
    convs = dout("convs", [DEPTH, 512, NSQ, 3]); lrus = dout("lrus", [DEPTH, 512, NSQ]); pools = dout("pools", [DEPTH, 256, NSQ, 15])

    KTd = nc.dram_tensor("KTd", [DEPTH, 64, 4, SEQ], BF16, kind="Internal").ap()
    Vd = nc.dram_tensor("Vd", [DEPTH, SEQ, 256], BF16, kind="Internal").ap()

    def sb(name, shape, dt=F32):
        return nc.alloc_sbuf_tensor(name, list(shape), dt).ap()

    x = sb("x", [128, 4, D])
    hT = sb("hT", [128, 8, T], BF16)
    mixT = sb("mixT", [128, 8, T], BF16)
    xa = sb("xa", [128, 4, 3 + T])
    ga = sb("ga", [128, 4, T], BF16)
    ub = sb("ub", [128, 2, 15 + T])
    QT = sb("QT", [64, 4, T], BF16)
    KTc = sb("KTc", [64, 4, T], BF16)
    Vc = sb("Vc", [128, 4, 256], BF16)
    qkv = sb("qkv", [128, 768])
    qkb = sb("qkb", [128, 512], BF16)
    wring = [sb(f"wr{i}", [128, 4096], BF16) for i in range(4)]
    wdring = [sb(f"wd{i}", [128, 2, D], BF16) for i in range(2)]
    gw = sb("gw", [128, 2, 4, 128], BF16)
    pw = sb("pw", [128, 2, 128], BF16)
    prms = sb("prms", [128, DEPTH, NPRM])
    cvec = sb("cvec", [128, DEPTH, 4])
    grows = sb("grows", [128, 2, D])
    qkgs = sb("qkgs", [128, 512])
    cst = sb("cst", [128, 768], BF16)
    icnt = sb("icnt", [128, 2, 16])
    TS = [sb(f"ts{i}", [128, 576]) for i in range(10)]
    e_t = sb("e_t", [128, 4, T]); nl_t = sb("nl_t", [128, 4, T], BF16); at_t = sb("at_t", [128, 4, T], BF16); R_t = sb("R_t", [128, 4, T], BF16)
    junk = sb("junk", [128, D]); hn = sb("hn", [128, D], BF16)
    st8 = sb("st8", [128, 16])
    kK = sb("kK", [64, 8192], BF16); kV = sb("kV", [128, 4096], BF16)
    ffT = [sb(f"ffT{i}", [128, 2, T], BF16) for i in range(2)]
    pconv = sb("pconv", [128, DEPTH, 4, 3]); plru = sb("plru", [128, DEPTH, 4]); ppool = sb("ppool", [128, DEPTH, 2, 15])
    sconv_s = sb("sconv_s", [128, 4, NSQ, 3]); slru_s = sb("slru_s", [128, 4, NSQ]); spool_s = sb("spool_s", [128, 2, NSQ, 15])
    hlast = sb("hlast", [128, 4, NSQ])

    PS = [nc.alloc_psum_tensor(f"ps{i}", [128, 512], F32).ap() for i in range(8)]

    negtri = cst[:, 0:128]; negones = cst[:, 128:256]; identb = cst[:, 256:384]; tri128 = cst[:, 384:512]
    maskS = [cst[:, 512:576], cst[:, 576:640]]; zerosb = cst[:, 640:768]

    S.dma("pool", cst, cstf, writes=["cst"])
    S.dma("sp", icnt, invcnt, writes=["icnt"])
    S.dma("sp", prms, prm.rearrange("l p n -> p l n"), writes=["prms"])
    for t_, key in ((pconv, "pconv"), (plru, "plru"), (ppool, "ppool")):
        S.op("dve", lambda e, t_=t_: e.memset(t_, 0.0), writes=[key])
    for l in range(DEPTH):
        for k in range(4):
            S.op("act", lambda e, l=l, k=k: e.activation(out=cvec[:, l, k:k + 1], in_=prms[:, l, k * 8 + 7:k * 8 + 8], func=AF.Exp, scale=-1.0),
                 reads=["prms"], writes=["cvec"])
    S.op("act", lambda e: e.activation(out=cvec, in_=cvec, func=AF.Ln, bias=1.0), reads=["cvec"], writes=["cvec"])
    S.op("dve", lambda e: e.tensor_scalar(out=cvec, in0=cvec, scalar1=-8.0, scalar2=None, op0=ALU.mult), reads=["cvec"], writes=["cvec"])

    wr_i = [0]

    def wslot():
        i = wr_i[0] % 4
        wr_i[0] += 1
        return i

    def load_layer_small(l):
        S.dma("pool", gw, gbd[l].rearrange("a k i j -> i a k j"), writes=["gw"])
        S.dma("pool", pw, pbd[l].rearrange("k i j -> i k j"), writes=["pw"])
        S.dma("sp", qkgs, qkg[l], writes=["qkgs"])

    def norm_phase(l, which, NT):
        S.dma("sp", grows[:, which, :], grow[l, which], writes=[f"grows{which}"])
        for i in range(NT):
            S.op("dve", lambda e: e.tensor_tensor(out=junk, in0=x[:, i, :], in1=x[:, i, :], op=ALU.mult), reads=[f"x{i}"], writes=["junk"])
            S.op("dve", lambda e: e.tensor_reduce(out=st8[:, 0:1], in_=junk, axis=AX.X, op=ALU.add), reads=["junk"], writes=["st8"])
            S.op("dve", lambda e: e.tensor_scalar(out=st8[:, 1:2], in0=st8[:, 0:1], scalar1=1.0 / D, scalar2=EPS, op0=ALU.mult, op1=ALU.add), reads=["st8"], writes=["st8"])
            S.op("act", lambda e: e.activation(out=st8[:, 3:4], in_=st8[:, 1:2], func=AF.Sqrt), reads=["st8"], writes=["st8"])
            S.op("dve", lambda e: e.reciprocal(out=st8[:, 2:3], in_=st8[:, 3:4]), reads=["st8"], writes=["st8"])
            S.op("dve", lambda e: e.scalar_tensor_tensor(out=hn, in0=x[:, i, :], scalar=st8[:, 2:3], in1=grows[:, which, :], op0=ALU.mult, op1=ALU.mult),
                 reads=["st8", f"x{i}", f"grows{which}"], writes=["hn"])
            pt = PS[i % 2].bitcast(BF16)
            for k in range(8):
                S.op("pe", lambda e, k=k: e.transpose(out=pt[:, k * 128:(k + 1) * 128], in_=hn[:, k * 128:(k + 1) * 128], identity=identb),
                     reads=["hn", "cst"], writes=[f"P{i % 2}"])
            S.op("act", lambda e: e.activation(out=hT[:, :, i * 128:(i + 1) * 128], in_=pt[:, 0:1024].rearrange("p (k t) -> p k t", k=8), func=AF.Copy),
                 reads=[f"P{i % 2}"], writes=["hT"])

    def load_w(slot, src, key_extra=()):
        S.dma("pool", slot, src, writes=list(key_extra))

    def proj_phase(l, seg):
        NT, N, nseq, L = seg["NT"], seg["N"], seg["nseq"], seg["L"]
        xa_v = xa[:, :, 0:nseq * (3 + L)].rearrange("p k (s c) -> p k s c", s=nseq)
        ub_v = ub[:, :, 0:nseq * (15 + L)].rearrange("p k (s c) -> p k s c", s=nseq)
        if seg["prompt"]:
            S.op("dve", lambda e: e.tensor_copy(out=xa[:, :, 0:3], in_=pconv[:, l, :, :]), reads=["pconv"], writes=["xa"])
            S.op("dve", lambda e: e.tensor_copy(out=ub[:, :, 0:15], in_=ppool[:, l, :, :]), reads=["ppool"], writes=["ub"])
        else:
            S.dma("sp", sconv_s, sconv[l].rearrange("(k p) s j -> p k s j", p=128), writes=["sconv_s"])
            S.dma("sp", slru_s, slru[l].rearrange("(k p) s -> p k s", p=128), writes=["slru_s"])
            S.dma("sp", spool_s, spool[l].rearrange("(k p) s j -> p k s j", p=128), writes=["spool_s"])
            S.op("dve", lambda e: e.tensor_copy(out=xa_v[:, :, :, 0:3], in_=sconv_s), reads=["sconv_s"], writes=["xa"])
            S.op("dve", lambda e: e.tensor_copy(out=ub_v[:, :, :, 0:15], in_=spool_s), reads=["spool_s"], writes=["ub"])
        wv = w_in[l].rearrange("(k p) e -> p k e", p=128)
        for g in range(4):
            si = wslot()
            slot = wring[si].rearrange("p (k e) -> p k e", k=8)
            S.dma("pool", slot, wv[:, :, g * 512:(g + 1) * 512], writes=[f"wr{si}", f"wr{si}b"])
            fm = {0: [0, 1, 2, 3], 1: [0, 1, 2, 3], 2: [0, 1], 3: []}[g]
            for c in fm:
                pb = 2 + (c % 2)
                ps = PS[pb]
                for kc in range(8):
                    S.op("pe", lambda e, kc=kc: e.matmul(ps[:, 0:N], lhsT=slot[:, kc, c * 128:(c + 1) * 128], rhs=hT[:, kc, 0:N], start=(kc == 0), stop=(kc == 7)),
                         reads=[f"wr{si}", "hT"], writes=[f"P{pb}"])
                if g == 0:
                    S.op("act", lambda e: e.activation(out=xa_v[:, c, :, 3:3 + L], in_=ps[:, 0:N].rearrange("p (s t) -> p s t", s=nseq), func=AF.Copy),
                         reads=[f"P{pb}"], writes=["xa"])
                elif g == 1:
                    g_, t_ = TS[0][:, 0:N], TS[1][:, 0:N]
                    S.op("act", lambda e: e.activation(out=g_, in_=ps[:, 0:N], func=AF.Copy), reads=[f"P{pb}"], writes=["ts0"])
                    S.op("dve", lambda e: e.tensor_tensor(out=t_, in0=g_, in1=g_, op=ALU.mult), reads=["ts0"], writes=["ts1"])
                    S.op("dve", lambda e: e.tensor_scalar(out=t_, in0=t_, scalar1=0.044715, scalar2=1.0, op0=ALU.mult, op1=ALU.add), reads=["ts1"], writes=["ts1"])
                    S.op("dve", lambda e: e.tensor_tensor(out=t_, in0=t_, in1=g_, op=ALU.mult), reads=["ts1", "ts0"], writes=["ts1"])
                    S.op("act", lambda e: e.activation(out=t_, in_=t_, func=AF.Sigmoid, scale=1.5957691216057308), reads=["ts1"], writes=["ts1"])
                    S.op("dve", lambda e: e.tensor_tensor(out=ga[:, c, 0:N], in0=t_, in1=g_, op=ALU.mult), reads=["ts1", "ts0"], writes=["ga"])
                else:
                    S.op("act", lambda e: e.activation(out=ub_v[:, c, :, 15:15 + L], in_=ps[:, 0:N].rearrange("p (s t) -> p s t", s=nseq), func=AF.Copy),
                         reads=[f"P{pb}"], writes=["ub"])
            if g == 2:
                slot2 = slot
                si2 = si
        slot3, si3 = slot, si
        import os
        KSUB = int(os.environ.get("KSUB", "9"))
        for i in range(NT if KSUB >= 2 else 0):
            pq, pkv = PS[4], PS[5]
            for kc in range(8):
                S.op("pe", lambda e, kc=kc: e.matmul(pq[:, 0:256], lhsT=hT[:, kc, i * 128:(i + 1) * 128], rhs=slot2[:, kc, 256:512], start=(kc == 0), stop=(kc == 7)),
                     reads=[f"wr{si2}", "hT"], writes=["P4"])
            for kc in range(8):
                S.op("pe", lambda e, kc=kc: e.matmul(pkv[:, 0:512], lhsT=hT[:, kc, i * 128:(i + 1) * 128], rhs=slot3[:, kc, 0:512], start=(kc == 0), stop=(kc == 7)),
                     reads=[f"wr{si3}", "hT"], writes=["P5"])
            S.op("act", lambda e: e.activation(out=qkv[:, 0:256], in_=pq[:, 0:256], func=AF.Copy), reads=["P4"], writes=["qkv"])
            S.op("act", lambda e: e.activation(out=qkv[:, 256:768], in_=pkv[:, 0:512], func=AF.Copy), reads=["P5"], writes=["qkv"])
            if KSUB < 3:
                continue
            sq = TS[2][:, 0:512]
            S.op("dve", lambda e: e.tensor_tensor(out=sq, in0=qkv[:, 0:512], in1=qkv[:, 0:512], op=ALU.mult), reads=["qkv"], writes=["ts2"])
            S.op("dve", lambda e: e.tensor_reduce(out=st8[:, 4:12], in_=sq.rearrange("p (h d) -> p h d", h=8), axis=AX.X, op=ALU.add), reads=["ts2"], writes=["st8"])
            S.op("dve", lambda e: e.tensor_scalar(out=st8[:, 4:12], in0=st8[:, 4:12], scalar1=1.0 / 64, scalar2=EPS, op0=ALU.mult, op1=ALU.add), reads=["st8"], writes=["st8"])
            S.op("act", lambda e: e.activation(out=st8[:, 4:12], in_=st8[:, 4:12], func=AF.Sqrt), reads=["st8"], writes=["st8"])
            S.op("dve", lambda e: e.reciprocal(out=st8[:, 4:12], in_=st8[:, 4:12]), reads=["st8"], writes=["st8"])
            S.op("dve", lambda e: e.tensor_tensor(out=qkv[:, 0:512].rearrange("p (h d) -> p h d", h=8), in0=qkv[:, 0:512].rearrange("p (h d) -> p h d", h=8),
                                                   in1=st8[:, 4:12].unsqueeze(2).to_broadcast([128, 8, 64]), op=ALU.mult), reads=["st8", "qkv"], writes=["qkv"])
            S.op("dve", lambda e: e.tensor_tensor(out=qkv[:, 0:512], in0=qkv[:, 0:512], in1=qkgs, op=ALU.mult), reads=["qkv", "qkgs"], writes=["qkv"])
            if KSUB < 4:
                continue
            r0 = seg["row0"] + i * 128
            kdst, vdst = (kp, vp) if seg["prompt"] else (ksm, vsm)
            S.dma("sp", kdst[l, r0:r0 + 128, :], qkv[:, 256:512], reads=["qkv"])
            S.dma("sp", vdst[l, r0:r0 + 128, :], qkv[:, 512:768], reads=["qkv"])
            S.op("act", lambda e: e.activation(out=qkb[:, 0:256], in_=qkv[:, 0:256], func=AF.Copy, scale=0.125), reads=["qkv"], writes=["qkb"])
            S.op("act", lambda e: e.activation(out=qkb[:, 256:512], in_=qkv[:, 256:512], func=AF.Copy), reads=["qkv"], writes=["qkb"])
            S.op("dve", lambda e: e.tensor_copy(out=Vc[:, i, :], in_=qkv[:, 512:768]), reads=["qkv"], writes=["Vc"])
            if KSUB < 5:
                continue
            for j in range(8):
                pj = PS[6 + j // 4]
                S.op("pe", lambda e, j=j, pj=pj: e.matmul(pj[0:64, (j % 4) * 128:(j % 4 + 1) * 128], lhsT=qkb[:, j * 64:(j + 1) * 64], rhs=identb, start=True, stop=True),
                     reads=["qkb", "cst"], writes=[f"P{6 + j // 4}"])
            S.op("act", lambda e: e.activation(out=QT[:, :, i * 128:(i + 1) * 128], in_=PS[6][0:64, :].rearrange("p (h t) -> p h t", h=4), func=AF.Copy),
                 reads=["P6"], writes=["QT"])
            S.op("dve", lambda e: e.tensor_copy(out=KTc[:, :, i * 128:(i + 1) * 128], in_=PS[7][0:64, :].rearrange("p (h t) -> p h t", h=4)),
                 reads=["P7"], writes=["KTc"])
        if seg["prompt"] and KSUB >= 6:
            j = seg["chunk"]
            S.dma("sp", KTd[l, :, :, j * T:(j + 1) * T], KTc, reads=["KTc"], writes=[f"KTd{l}_{j}"])
            S.dma("sp", Vd[l, j * T:(j + 1) * T, :].rearrange("(b s) f -> s b f", s=128), Vc, reads=["Vc"], writes=[f"Vd{l}_{j}"])

    def lru_phase(l, seg):
        NT, N, nseq, L = seg["NT"], seg["N"], seg["nseq"], seg["L"]
        xa_v = xa[:, :, 0:nseq * (3 + L)].rearrange("p k (s c) -> p k s c", s=nseq)

        def v3(t):
            return t[:, 0:N].rearrange("p (s t) -> p s t", s=nseq)
        for k in range(4):
            pc = lambda j: prms[:, l, k * 8 + j:k * 8 + j + 1]
            xc, xcb = TS[0], TS[1].bitcast(BF16)
            S.op("dve", lambda e: e.tensor_scalar(out=v3(xc), in0=xa_v[:, k, :, 0:L], scalar1=pc(0), scalar2=pc(4), op0=ALU.mult, op1=ALU.add),
                 reads=["xa", "prms"], writes=["ts0"])
            for j in range(1, 4):
                S.op("dve", lambda e, j=j: e.scalar_tensor_tensor(out=v3(xc), in0=xa_v[:, k, :, j:j + L], scalar=pc(j), in1=v3(xc), op0=ALU.mult, op1=ALU.add),
                     reads=["xa", "ts0", "prms"], writes=["ts0"])
            S.op("act", lambda e: e.activation(out=xcb[:, 0:N], in_=xc[:, 0:N], func=AF.Copy), reads=["ts0"], writes=["ts1"])
            S.op("pe", lambda e: e.matmul(PS[2][:, 0:N], lhsT=gw[:, 0, k, :], rhs=xcb[:, 0:N], start=True, stop=True), reads=["gw", "ts1"], writes=["P2"])
            S.op("pe", lambda e: e.matmul(PS[3][:, 0:N], lhsT=gw[:, 1, k, :], rhs=xcb[:, 0:N], start=True, stop=True), reads=["gw", "ts1"], writes=["P3"])
            r_, i_, a_, u_, h_ = TS[2], TS[3], TS[4], TS[5], TS[6]
            S.op("act", lambda e: e.activation(out=r_[:, 0:N], in_=PS[2][:, 0:N], func=AF.Sigmoid, bias=pc(5)), reads=["P2", "prms"], writes=["ts2"])
            S.op("act", lambda e: e.activation(out=i_[:, 0:N], in_=PS[3][:, 0:N], func=AF.Sigmoid, bias=pc(6)), reads=["P3", "prms"], writes=["ts3"])
            S.op("act", lambda e: e.activation(out=a_[:, 0:N], in_=r_[:, 0:N], func=AF.Exp, scale=cvec[:, l, k:k + 1]), reads=["ts2", "cvec"], writes=["ts4"])
            S.op("dve", lambda e: e.tensor_tensor(out=u_[:, 0:N], in0=a_[:, 0:N], in1=a_[:, 0:N], op=ALU.mult), reads=["ts4"], writes=["ts5"])
            S.op("dve", lambda e: e.tensor_scalar(out=u_[:, 0:N], in0=u_[:, 0:N], scalar1=-1.0, scalar2=1.0, op0=ALU.mult, op1=ALU.add), reads=["ts5"], writes=["ts5"])
            S.op("act", lambda e: e.activation(out=u_[:, 0:N], in_=u_[:, 0:N], func=AF.Sqrt), reads=["ts5"], writes=["ts5"])
            S.op("dve", lambda e: e.tensor_tensor(out=i_[:, 0:N], in0=i_[:, 0:N], in1=xc[:, 0:N], op=ALU.mult), reads=["ts3", "ts0"], writes=["ts3"])
            S.op("dve", lambda e: e.tensor_tensor(out=u_[:, 0:N], in0=u_[:, 0:N], in1=i_[:, 0:N], op=ALU.mult), reads=["ts5", "ts3"], writes=["ts5"])
            for s in range(nseq):
                init = plru[:, l, k:k + 1] if seg["prompt"] else slru_s[:, k, s:s + 1]
                S.op("dve", lambda e, s=s, init=init: e.tensor_tensor_scan(out=h_[:, s * L:(s + 1) * L], data0=a_[:, s * L:(s + 1) * L], data1=u_[:, s * L:(s + 1) * L],
                                                                           initial=init, op0=ALU.mult, op1=ALU.add),
                     reads=["ts4", "ts5", "plru", "slru_s"], writes=["ts6"])
            if seg["prompt"]:
                S.op("dve", lambda e: e.tensor_copy(out=plru[:, l, k:k + 1], in_=h_[:, L - 1:L]), reads=["ts6"], writes=["plru"])
            else:
                S.op("dve", lambda e: e.tensor_copy(out=hlast[:, k, :], in_=h_[:, L - 1:N:L]), reads=["ts6"], writes=["hlast"])
            S.op("dve", lambda e: e.tensor_tensor(out=mixT[:, k, 0:N], in0=h_[:, 0:N], in1=ga[:, k, 0:N], op=ALU.mult), reads=["ts6", "ga"], writes=["mixT"])
        if seg["prompt"]:
            S.op("dve", lambda e: e.tensor_copy(out=pconv[:, l, :, :], in_=xa[:, :, L:L + 3]), reads=["xa"], writes=["pconv"])
        else:
            for k in range(4):
                S.dma("sp", convs[l, k * 128:(k + 1) * 128], xa_v[:, k, :, L:L + 3], reads=["xa"])
            S.dma("sp", lrus[l].rearrange("(k p) s -> p k s", p=128), hlast, reads=["hlast"])

    def pool_phase(l, seg):
        NT, N, nseq, L = seg["NT"], seg["N"], seg["nseq"], seg["L"]
        W_ = nseq * (15 + L)
        ub_v = ub[:, :, 0:W_].rearrange("p k (s c) -> p k s c", s=nseq)
        for k in range(2):
            pa = TS[7][:, 0:W_].rearrange("p (s c) -> p s c", s=nseq)
            pb = TS[8][:, 0:W_].rearrange("p (s c) -> p s c", s=nseq)
            C = 15 + L
            u_ = ub_v[:, k]
            S.op("dve", lambda e: e.tensor_tensor(out=pa[:, :, 1:C], in0=u_[:, :, 1:C], in1=u_[:, :, 0:C - 1], op=ALU.add), reads=["ub"], writes=["ts7"])
            S.op("dve", lambda e: e.tensor_tensor(out=pb[:, :, 3:C], in0=pa[:, :, 3:C], in1=pa[:, :, 1:C - 2], op=ALU.add), reads=["ts7"], writes=["ts8"])
            if k == 1:
                S.op("dve", lambda e: e.tensor_tensor(out=pa[:, :, 7:C], in0=pb[:, :, 7:C], in1=pb[:, :, 3:C - 4], op=ALU.add), reads=["ts8"], writes=["ts7"])
                S.op("dve", lambda e: e.tensor_tensor(out=pb[:, :, 15:C], in0=pa[:, :, 15:C], in1=pa[:, :, 7:C - 8], op=ALU.add), reads=["ts7"], writes=["ts8"])
            ws = (2, 4) if k == 0 else (8, 16)
            d_ = TS[9].bitcast(BF16)[:, 0:N].rearrange("p (s t) -> p s t", s=nseq)
            for half, (src, w) in enumerate(((pa, ws[0]), (pb, ws[1]))):
                p0, p1 = half * 64, half * 64 + 64
                S.op("dve", lambda e, src=src, w=w, p0=p0, p1=p1: e.scalar_tensor_tensor(out=d_[p0:p1], in0=src[p0:p1, :, 15:C], scalar=1.0 / w, in1=u_[p0:p1, :, 15:C],
                                                                                       op0=ALU.mult, op1=ALU.subtract), reads=["ts7", "ts8", "ub"], writes=["ts9"])
                if seg["prompt"] and seg["chunk"] == 0:
                    tmp = st8[p0:p1, 0:16]
                    S.op("dve", lambda e, src=src, p0=p0, p1=p1, tmp=tmp: e.tensor_tensor(out=tmp, in0=src[p0:p1, 0, 15:31], in1=icnt[p0:p1, k, :], op=ALU.mult),
                         reads=["ts7", "ts8", "icnt"], writes=["st8"])
                    S.op("dve", lambda e, p0=p0, p1=p1, tmp=tmp: e.tensor_tensor(out=d_[p0:p1, 0, 0:16], in0=tmp, in1=u_[p0:p1, 0, 15:31], op=ALU.subtract),
                         reads=["st8", "ub"], writes=["ts9"])
            S.op("pe", lambda e: e.matmul(PS[2][:, 0:N], lhsT=pw[:, k, :], rhs=TS[9].bitcast(BF16)[:, 0:N], start=True, stop=True), reads=["pw", "ts9"], writes=["P2"])
            S.op("act", lambda e: e.activation(out=mixT[:, 4 + k, 0:N], in_=PS[2][:, 0:N], func=AF.Copy, scale=prms[:, l, 32 + k:33 + k]), reads=["P2", "prms"], writes=["mixT"])
        if seg["prompt"]:
            S.op("dve", lambda e: e.tensor_copy(out=ppool[:, l, :, :], in_=ub[:, :, L:L + 15]), reads=["ub"], writes=["ppool"])
        else:
            for k in range(2):
                S.dma("sp", pools[l, k * 128:(k + 1) * 128], ub_v[:, k, :, L:L + 15], reads=["ub"])

    zps = [PS[0], PS[1], PS[2], PS[3]]
    ops_ = [PS[4], PS[5], PS[6], PS[7]]
    zkeys = ["P0", "P1", "P2", "P3"]
    okeys = ["P4", "P5", "P6", "P7"]

    def att_unit(KT, KTkeys, V, Vkeys, nk, c0, c1, mask, first, last):
        w = c1 - c0
        for h in range(4):
            S.op("pe", lambda e, h=h: e.matmul(zps[h][0:nk, c0:c1], lhsT=KT[:, h, :], rhs=QT[:, h, c0:c1], start=True, stop=False),
                 reads=KTkeys + ["QT"], writes=[zkeys[h]])
        zall = [z[0:nk, c0:c1] for z in zps]
        for h in range(4):
            S.op("act", lambda e, h=h: e.activation(out=e_t[0:nk, h, c0:c1], in_=zall[h], func=AF.Exp), reads=[zkeys[h]], writes=[f"e{h}"])
        S.op("act", lambda e: e.activation(out=nl_t[0:nk, :, c0:c1], in_=e_t[0:nk, :, c0:c1], func=AF.Ln, bias=1.0), reads=["e0", "e1", "e2", "e3"], writes=["nl"])
        if mask is not None:
            mw = mask.shape[1]
            S.op("dve", lambda e: e.tensor_tensor(out=nl_t[0:nk, :, c0:c0 + mw], in0=nl_t[0:nk, :, c0:c0 + mw],
                                                   in1=mask.unsqueeze(1).to_broadcast([nk, 4, mw]), op=ALU.mult), reads=["nl", "cst"], writes=["nl"])
        for h in range(4):
            S.op("pe", lambda e, h=h: e.matmul(zps[h][0:nk, c0:c1], lhsT=negtri[0:nk, 0:nk], rhs=nl_t[0:nk, h, c0:c1], start=False, stop=first),
                 reads=["nl", "cst"], writes=[zkeys[h]])
            if not first:
                S.op("pe", lambda e, h=h: e.matmul(zps[h][0:nk, c0:c1], lhsT=negones[:, 0:nk], rhs=R_t[:, h, c0:c1], start=False, stop=True),
                     reads=["R", "cst"], writes=[zkeys[h]])
        for h in range(4):
            S.op("act", lambda e, h=h: e.activation(out=at_t[0:nk, h, c0:c1], in_=zall[h], func=AF.Exp), reads=[zkeys[h]], writes=[f"at{h}"])
        if mask is not None:
            mw = mask.shape[1]
            S.op("dve", lambda e: e.tensor_tensor(out=at_t[0:nk, :, c0:c0 + mw], in0=at_t[0:nk, :, c0:c0 + mw],
                                                   in1=mask.unsqueeze(1).to_broadcast([nk, 4, mw]), op=ALU.mult), reads=["at0", "at1", "at2", "at3", "cst"],
                 writes=["at0", "at1", "at2", "at3"])
        if not last:
            S.op("dve", lambda e: e.tensor_tensor(out=R_t[0:nk, :, c0:c1], in0=R_t[0:nk, :, c0:c1], in1=nl_t[0:nk, :, c0:c1], op=ALU.add), reads=["nl", "R"], writes=["R"])
        for h in range(4):
            if h % 2 == 0:
                S.op("pe", lambda e, h=h: e.matmul(ops_[h][0:64, c0:c1], lhsT=V[0:nk, h * 64:(h + 1) * 64], rhs=at_t[0:nk, h, c0:c1], start=False, stop=last),
                     reads=Vkeys + [f"at{h}"], writes=[okeys[h]])
            else:
                S.op("pe", lambda e, h=h: e.matmul(ops_[h][0:128, c0:c1], lhsT=V[0:nk, (h - 1) * 64:(h + 1) * 64], rhs=at_t[0:nk, h, c0:c1], start=False, stop=last),
                     reads=Vkeys + [f"at{h}"], writes=[okeys[h]])

    def att_begin(c0, c1):
        S.op("dve", lambda e: e.memset(R_t[:, :, c0:c1], 0.0), writes=["R"])
        for h in range(4):
            S.op("pe", lambda e, h=h: e.matmul(ops_[h][:, c0:c1], lhsT=zerosb, rhs=hT[:, 0, c0:c1], start=True, stop=False), reads=["cst", "hT"], writes=[okeys[h]])

    def att_end(c0, c1):
        for h in range(4):
            p0 = (h % 2) * 64
            S.op("act", lambda e, h=h, p0=p0: e.activation(out=mixT[p0:p0 + 64, 6 + h // 2, c0:c1], in_=ops_[h][p0:p0 + 64, c0:c1], func=AF.Copy),
                 reads=[okeys[h]], writes=["mixT"])

    def attn_prompt(l, j):
        att_begin(0, T)
        nhist = 4 * j
        units = []
        for jj in (3, 2, 1, 0):
            units.append(("cur", jj))
        npieces = (nhist + 7) // 8
        for p in range(npieces - 1, -1, -1):
            nb = min(8, nhist - 8 * p)
            for b in range(nb - 1, -1, -1):
                units.append(("hist", p, b, nb))
        loaded = {}
        pslot = [0]
        for ui, u in enumerate(units):
            first, last = ui == 0, ui == len(units) - 1
            if u[0] == "cur":
                jj = u[1]
                att_unit(KTc[:, :, jj * 128:(jj + 1) * 128], ["KTc"], Vc[:, jj, :], ["Vc"], 128, jj * 128, T, tri128, first, last)
            else:
                _, p, b, nb = u
                if p not in loaded:
                    s = pslot[0] % 2
                    pslot[0] += 1
                    Kv = kK[:, s * 4096:(s + 1) * 4096].rearrange("p (h t) -> p h t", h=4)
                    Vv = kV[:, s * 2048:(s + 1) * 2048].rearrange("p (b f) -> p b f", b=8)
                    t0 = p * 1024
                    rk = [f"KTd{l}_{c}" for c in range(2 * p, min(2 * p + 2, j))]
                    rv = [f"Vd{l}_{c}" for c in range(2 * p, min(2 * p + 2, j))]
                    S.dma("sp", Kv[:, :, 0:nb * 128], KTd[l, :, :, t0:t0 + nb * 128], reads=rk, writes=[f"kvK{s}"])
                    S.dma("sp", Vv[:, 0:nb, :], Vd[l, t0:t0 + nb * 128, :].rearrange("(b s) f -> s b f", s=128), reads=rv, writes=[f"kvV{s}"])
                    loaded[p] = (Kv, Vv, s)
                Kv, Vv, s = loaded[p]
                att_unit(Kv[:, :, b * 128:(b + 1) * 128], [f"kvK{s}"], Vv[:, b, :], [f"kvV{s}"], 128, 0, T, None, first, last)
        att_end(0, T)

    def attn_sample(l):
        Kv = kK.rearrange("p (h t) -> p h t", h=4)
        Vv = kV.rearrange("p (b f) -> p b f", b=16)
        for s in range(NSQ):
            c0, c1 = s * LS, (s + 1) * LS
            S.dma("pool", Kv[:, :, 0:PAST], ckT[l, s], writes=["kvK0", "kvK1"])
            S.dma("pool", Vv[:, 0:NPB, :], cv[l, s].rearrange("(b s) f -> s b f", s=128), writes=["kvV0", "kvV1"])
            att_begin(c0, c1)
            tl = s // 2
            att_unit(KTc[:, :, tl * 128:(tl + 1) * 128], ["KTc"], Vc[:, tl, :], ["Vc"], 128, c0, c1, maskS[s % 2], True, False)
            for b in range(NPB - 1, -1, -1):
                att_unit(Kv[:, :, b * 128:(b + 1) * 128], ["kvK0", "kvK1"], Vv[:, b, :], ["kvV0", "kvV1"], 128, c0, c1, None, False, b == 0)
            att_end(c0, c1)

    def wout_phase(l, seg):
        NT = seg["NT"]
        wv = w_out[l].rearrange("(k p) e -> p k e", p=128)
        for dh in range(2):
            si = wslot()
            slot = wring[si].rearrange("p (k e) -> p k e", k=8)
            S.dma("pool", slot, wv[:, :, dh * 512:(dh + 1) * 512], writes=[f"wr{si}", f"wr{si}b"])
            for i in range(NT):
                pb = i % 2
                for kc in range(8):
                    S.op("pe", lambda e, kc=kc: e.matmul(PS[pb], lhsT=mixT[:, kc, i * 128:(i + 1) * 128], rhs=slot[:, kc, :], start=(kc == 0), stop=(kc == 7)),
                         reads=["mixT", f"wr{si}"], writes=[f"P{pb}"])
                S.op("dve", lambda e: e.tensor_tensor(out=x[:, i, dh * 512:(dh + 1) * 512], in0=x[:, i, dh * 512:(dh + 1) * 512], in1=PS[pb], op=ALU.add),
                     reads=[f"P{pb}", f"x{i}"], writes=[f"x{i}"])

    def ffn_phase(l, seg):
        NT, N = seg["NT"], seg["N"]
        wg = w_gate[l].rearrange("(k p) f -> p k f", p=128)
        wu = w_up[l].rearrange("(k p) f -> p k f", p=128)
        wd = w_down[l].rearrange("(c p) d -> p c d", p=128)
        for fg in range(DFF // 256):
            si = wslot()
            slot = wring[si].rearrange("p (a k f) -> p a k f", a=2, k=8)
            S.dma("pool", slot[:, 0], wg[:, :, fg * 256:(fg + 1) * 256], writes=[f"wr{si}", f"wr{si}b"])
            S.dma("pool", slot[:, 1], wu[:, :, fg * 256:(fg + 1) * 256], writes=[f"wr{si}b"])
            di = fg % 2
            S.dma("pool", wdring[di], wd[:, fg * 2:fg * 2 + 2, :], writes=[f"wd{di}"])
            ff = ffT[fg % 2]
            for fc in range(2):
                pg, pu = PS[2 + fc * 2], PS[3 + fc * 2]
                for kc in range(8):
                    S.op("pe", lambda e, kc=kc: e.matmul(pg[:, 0:N], lhsT=slot[:, 0, kc, fc * 128:(fc + 1) * 128], rhs=hT[:, kc, 0:N], start=(kc == 0), stop=(kc == 7)),
                         reads=[f"wr{si}", "hT"], writes=[f"P{2 + fc * 2}"])
                for kc in range(8):
                    S.op("pe", lambda e, kc=kc: e.matmul(pu[:, 0:N], lhsT=slot[:, 1, kc, fc * 128:(fc + 1) * 128], rhs=hT[:, kc, 0:N], start=(kc == 0), stop=(kc == 7)),
                         reads=[f"wr{si}b", "hT"], writes=[f"P{3 + fc * 2}"])
                sg = TS[fc][:, 0:N]
                S.op("act", lambda e: e.activation(out=sg, in_=pg[:, 0:N], func=AF.Silu), reads=[f"P{2 + fc * 2}"], writes=[f"ts{fc}"])
                S.op("dve", lambda e: e.tensor_tensor(out=ff[:, fc, 0:N], in0=sg, in1=pu[:, 0:N], op=ALU.mult), reads=[f"ts{fc}", f"P{3 + fc * 2}"], writes=[f"ff{fg % 2}"])
            for i in range(NT):
                for dh in range(2):
                    pb = (i * 2 + dh) % 2
                    for fc in range(2):
                        S.op("pe", lambda e, fc=fc: e.matmul(PS[pb], lhsT=ff[:, fc, i * 128:(i + 1) * 128], rhs=wdring[di][:, fc, dh * 512:(dh + 1) * 512], start=(fc == 0), stop=(fc == 1)),
                             reads=[f"ff{fg % 2}", f"wd{di}"], writes=[f"P{pb}"])
                    S.op("dve", lambda e: e.tensor_tensor(out=x[:, i, dh * 512:(dh + 1) * 512], in0=x[:, i, dh * 512:(dh + 1) * 512], in1=PS[pb], op=ALU.add),
                         reads=[f"P{pb}", f"x{i}"], writes=[f"x{i}"])

    def run_segment(seg):
        NT = seg["NT"]
        src = xp if seg["prompt"] else xs
        r0 = seg["row0"]
        for i in range(NT):
            S.dma("sp", x[:, i, :], src[r0 + i * 128:r0 + (i + 1) * 128, :], writes=[f"x{i}"])
        import os
        KS = int(os.environ.get("KSTOP", "9"))
        for l in range(DEPTH):
            load_layer_small(l)
            norm_phase(l, 0, NT)
            if KS >= 2:
                proj_phase(l, seg)
            if KS >= 3:
                lru_phase(l, seg)
            if KS >= 4:
                pool_phase(l, seg)
            if KS >= 5:
                if seg["prompt"]:
                    attn_prompt(l, seg["chunk"])
                else:
                    attn_sample(l)
            if KS >= 6:
                wout_phase(l, seg)
            if KS >= 7:
                norm_phase(l, 1, NT)
                ffn_phase(l, seg)
        dst = yp if seg["prompt"] else ys
        for i in range(NT):
            S.dma("sp", dst[r0 + i * 128:r0 + (i + 1) * 128, :], x[:, i, :], reads=[f"x{i}"])

    for j in range(NCH):
        run_segment(dict(prompt=True, chunk=j, NT=4, N=T, nseq=1, L=T, row0=j * T))
    for l in range(DEPTH):
        S.dma("sp", convp[l].rearrange("(k p) j -> p k j", p=128), pconv[:, l, :, :], reads=["pconv"])
        S.dma("sp", lrup[l].rearrange("(k p) o -> p k o", p=128), plru[:, l, :].unsqueeze(2), reads=["plru"], allow_slow_non_contiguous=True)
        S.dma("sp", poolp[l].rearrange("(k p) j -> p k j", p=128), ppool[:, l, :, :], reads=["ppool"])
    import os
    if os.environ.get("KSAMPLE", "1") == "1":
        run_segment(dict(prompt=False, chunk=0, NT=2, N=NSQ * LS, nseq=NSQ, L=LS, row0=0))
    S.finish()
    return nc, S


def _consts():
    c = np.zeros((128, 768), np.float32)
    j = np.arange(128)[:, None]; s = np.arange(128)[None, :]
    c[:, 0:128] = -(j >= s).astype(np.float32)
    c[:, 128:256] = -1.0
    c[:, 256:384] = np.eye(128, dtype=np.float32)
    c[:, 384:512] = (j < s).astype(np.float32)
    t64 = (np.arange(64)[:, None] < np.arange(64)[None, :]).astype(np.float32)
    c[0:64, 512:576] = t64
    c[64:128, 576:640] = t64
    return c


def _invcnt():
    ic = np.zeros((128, 2, 16), np.float32)
    ws = {(0, 0): 2, (0, 1): 4, (1, 0): 8, (1, 1): 16}
    pos = np.arange(16)
    for (k, half), w in ws.items():
        ic[half * 64:(half + 1) * 64, k, :] = 1.0 / np.minimum(pos + 1, w)
    return ic


def _blockdiag(w):
    Ld, nb = w.shape[0], w.shape[1]
    out = np.zeros((Ld, nb // 2, 128, 128), np.float32)
    for b in range(nb):
        k, h = b // 2, b % 2
        out[:, k, h * 64:(h + 1) * 64, h * 64:(h + 1) * 64] = w[:, b]
    return out


_CACHE = {}


def kernel(x_prompt, x_sample, cache_k, cache_v, state_conv, state_lru, state_pool,
           attn_norm, w_in, conv_w, conv_b, gate_a_w, gate_a_b, gate_x_w, gate_x_b,
           lru_lambda, pool_w, pool_scale, q_norm, k_norm, w_out, ffn_norm,
           w_gate, w_up, w_down):
    cfg = dict(CFG)
    SEQ, DEPTH, PAST = cfg["SEQ"], cfg["DEPTH"], cfg["PAST"]
    f = lambda a: np.ascontiguousarray(np.asarray(a, dtype=np.float32))
    key = (SEQ, DEPTH, PAST)
    if key not in _CACHE:
        _CACHE[key] = build_program(cfg)
    nc, S = _CACHE[key]
    prm = np.zeros((DEPTH, 128, NPRM), np.float32)
    cw = f(conv_w); cb = f(conv_b); gab = f(gate_a_b).reshape(DEPTH, 512); gxb = f(gate_x_b).reshape(DEPTH, 512); lam = f(lru_lambda); psc = f(pool_scale)
    for k in range(4):
        sl = slice(k * 128, (k + 1) * 128)
        for j in range(4):
            prm[:, :, k * 8 + j] = cw[:, j, sl]
        prm[:, :, k * 8 + 4] = cb[:, sl]
        prm[:, :, k * 8 + 5] = gab[:, sl]
        prm[:, :, k * 8 + 6] = gxb[:, sl]
        prm[:, :, k * 8 + 7] = lam[:, sl]
    for k in range(2):
        prm[:, :, 32 + k] = psc[:, k * 128:(k + 1) * 128]
    grow = np.ascontiguousarray(np.broadcast_to(np.stack([f(attn_norm), f(ffn_norm)], 1)[:, :, None, :], (DEPTH, 2, 128, D)))
    qkg = np.concatenate([np.tile(f(q_norm), (1, 4)), np.tile(f(k_norm), (1, 4))], 1)
    qkg = np.ascontiguousarray(np.broadcast_to(qkg[:, None, :], (DEPTH, 128, 512)))
    gbd = np.ascontiguousarray(np.stack([_blockdiag(f(gate_a_w)), _blockdiag(f(gate_x_w))], 1))
    pbd = _blockdiag(f(pool_w))
    common = dict(xp=f(x_prompt)[0], w_in=f(w_in), w_out=f(w_out), w_gate=f(w_gate), w_up=f(w_up), w_down=f(w_down),
                  gbd=gbd, pbd=pbd, prm=prm, grow=grow, qkg=qkg, cstf=_consts(), invcnt=_invcnt())
    ck = f(cache_k); cvv = f(cache_v); sc = f(state_conv); sl_ = f(state_lru); spl = f(state_pool); xsm = f(x_sample)
    in_maps = []
    for c in range(8):
        b0, b1 = c * NSQ, (c + 1) * NSQ
        m = dict(common)
        m["xs"] = np.ascontiguousarray(xsm[b0:b1].reshape(NSQ * LS, D))
        m["ckT"] = np.ascontiguousarray(ck[:, b0:b1].transpose(0, 1, 4, 3, 2))
        m["cv"] = np.ascontiguousarray(cvv[:, b0:b1].reshape(DEPTH, NSQ, PAST, 256))
        m["sconv"] = np.ascontiguousarray(sc[:, b0:b1].transpose(0, 3, 1, 2))
        m["slru"] = np.ascontiguousarray(sl_[:, b0:b1].transpose(0, 2, 1))
        m["spool"] = np.ascontiguousarray(spl[:, b0:b1].transpose(0, 3, 1, 2))
        in_maps.append(m)
    res = run_bass_kernel_spmd(nc, in_maps, core_ids=list(range(8)))
    R = res.results
    r0 = R[0]
    y_prompt = r0["yp"].reshape(1, SEQ, D)
    nkp = r0["kp"].reshape(DEPTH, 1, SEQ, 4, 64); nvp = r0["vp"].reshape(DEPTH, 1, SEQ, 4, 64)
    ncp = r0["convp"].transpose(0, 2, 1).reshape(DEPTH, 1, 3, 512)
    nhp = r0["lrup"].reshape(DEPTH, 1, 512)
    npp = r0["poolp"].transpose(0, 2, 1).reshape(DEPTH, 1, 15, 256)
    y_sample = np.concatenate([R[c]["ys"].reshape(NSQ, LS, D) for c in range(8)], 0)
    nks = np.concatenate([R[c]["ksm"].reshape(DEPTH, NSQ, LS, 4, 64) for c in range(8)], 1)
    nvs = np.concatenate([R[c]["vsm"].reshape(DEPTH, NSQ, LS, 4, 64) for c in range(8)], 1)
    ncs = np.concatenate([R[c]["convs"].transpose(0, 2, 3, 1) for c in range(8)], 1)
    nhs = np.concatenate([R[c]["lrus"].transpose(0, 2, 1) for c in range(8)], 1)
    nps = np.concatenate([R[c]["pools"].transpose(0, 2, 3, 1) for c in range(8)], 1)
    outs = (y_prompt, y_sample, nkp, nvp, ncp, nhp, npp, nks, nvs, ncs, nhs, nps)
    return tuple(np.ascontiguousarray(o, dtype=np.float32) for o in outs)
```

```python
import numpy as np
import concourse.bass as bass
import concourse.mybir as mybir
from concourse.bass_utils import run_bass_kernel_spmd

F32 = mybir.dt.float32
BF16 = mybir.dt.bfloat16
AF = mybir.ActivationFunctionType
ALU = mybir.AluOpType
AX = mybir.AxisListType

CFG = {"SEQ": 16384, "DEPTH": 4, "PAST": 2048}
D = 1024
DIN = 2048
DFF = 2816
T = 512
NSQ = 4
LS = 64
EPS = 1e-6
NPRM = 34
NDMA = 6


class Sched:
    def __init__(self, nc):
        self.nc = nc
        self.eng = {"pe": nc.tensor, "act": nc.scalar, "dve": nc.vector, "pool": nc.gpsimd, "sp": nc.sync}
        self.sem = {e: nc.alloc_semaphore(f"sem_{e}") for e in self.eng}
        self.cnt = {e: 0 for e in self.eng}
        self.seen = {e: {} for e in self.eng}
        self.last_w = {}
        self.readers = {}
        self.dsem = {q: [[nc.alloc_semaphore(f"dsem_{q}{i}"), 0] for i in range(NDMA)] for q in ("sp", "pool")}
        self.dnext = {"sp": 0, "pool": 0}
        self.ninstr = 0

    def _wait(self, e, deps, extra=None):
        eng = self.eng[e]
        best = {}
        for h in deps:
            if h is None:
                continue
            key, val = h
            if e == "pe" and key == "pe":
                continue
            if best.get(key, 0) < val:
                best[key] = val
        need = []
        for key, val in best.items():
            if self.seen[e].get(key, 0) >= val:
                continue
            sem = self.sem[key] if isinstance(key, str) else self.dsem[key[0]][key[1]][0]
            need.append((sem, val))
            self.seen[e][key] = val
        if extra is not None:
            need.append(extra)
        for sem, val in need[:-1]:
            eng.wait_ge(sem, val)
            self.ninstr += 1
        return need[-1] if need else None

    def _deps(self, reads, writes):
        deps = []
        for k in reads:
            deps.append(self.last_w.get(k))
        for k in writes:
            deps.append(self.last_w.get(k))
            deps.extend(self.readers.get(k, {}).values())
        return deps

    def _record(self, h, reads, writes):
        for k in writes:
            self.last_w[k] = h
            self.readers[k] = {}
        for k in reads:
            self.readers.setdefault(k, {})[h[0]] = h

    def op(self, e, emit, reads=(), writes=()):
        emb = self._wait(e, self._deps(reads, writes))
        ins = emit(self.eng[e])
        if emb is not None:
            ins._wait_ge(emb[0], emb[1])
        self.cnt[e] += 1
        ins.then_inc(self.sem[e], 1)
        self.ninstr += 1
        h = (e, self.cnt[e])
        self._record(h, reads, writes)
        return h

    def dma(self, q, out, in_, reads=(), writes=(), **kw):
        i = self.dnext[q] % NDMA
        self.dnext[q] += 1
        slot = self.dsem[q][i]
        extra = None
        if slot[1] > 0 and self.seen[q].get((q, i), 0) < slot[1]:
            extra = (slot[0], slot[1])
            self.seen[q][(q, i)] = slot[1]
        emb = self._wait(q, self._deps(reads, writes), extra)
        if emb is not None:
            self.eng[q].wait_ge(emb[0], emb[1])
            self.ninstr += 1
        ins = self.eng[q].dma_start(out=out, in_=in_, **kw)
        slot[1] += 16
        ins.then_inc(slot[0], 16)
        self.ninstr += 1
        h = ((q, i), slot[1])
        self._record(h, reads, writes)
        return h

    def finish(self):
        for q in ("sp", "pool"):
            for i, slot in enumerate(self.dsem[q]):
                if slot[1] > 0:
                    self.eng[q].wait_ge(slot[0], slot[1])


def build_program(cfg):
    SEQ, DEPTH, PAST = cfg["SEQ"], cfg["DEPTH"], cfg["PAST"]
    NCH = SEQ // T
    NPB = PAST // 128
    nc = bass.Bass("TRN2", target_bir_lowering=False)
    S = Sched(nc)

    def din(name, shape, dt=F32):
        return nc.dram_tensor(name, list(shape), dt, kind="ExternalInput").ap()

    def dout(name, shape, dt=F32):
        return nc.dram_tensor(name, list(shape), dt, kind="ExternalOutput").ap()

    xp = din("xp", [SEQ, D]); xs = din("xs", [NSQ * LS, D])
    ckT = din("ckT", [DEPTH, NSQ, 64, 4, PAST]); cv = din("cv", [DEPTH, NSQ, PAST, 256])
    sconv = din("sconv", [DEPTH, 512, NSQ, 3]); slru = din("slru", [DEPTH, 512, NSQ]); spool = din("spool", [DEPTH, 256, NSQ, 15])
    w_in = din("w_in", [DEPTH, D, DIN]); w_out = din("w_out", [DEPTH, D, D])
    w_gate = din("w_gate", [DEPTH, D, DFF]); w_up = din("w_up", [DEPTH, D, DFF]); w_down = din("w_down", [DEPTH, DFF, D])
    gbd = din("gbd", [DEPTH, 2, 4, 128, 128]); pbd = din("pbd", [DEPTH, 2, 128, 128])
    prm = din("prm", [DEPTH, 128, NPRM]); grow = din("grow", [DEPTH, 2, 128, D]); qkg = din("qkg", [DEPTH, 128, 512])
    cstf = din("cstf", [128, 768]); invcnt = din("invcnt", [128, 2, 16])

    yp = dout("yp", [SEQ, D]); ys = dout("ys", [NSQ * LS, D])
    kp = dout("kp", [DEPTH, SEQ, 256]); vp = dout("vp", [DEPTH, SEQ, 256])
    convp = dout("convp", [DEPTH, 512, 3]); lrup = dout("lrup", [DEPTH, 512, 1]); poolp = dout("poolp", [DEPTH, 256, 15])
    ksm = dout("ksm", [DEPTH, NSQ * LS, 256]); vsm = dout("vsm", [DEPTH, NSQ * LS, 256])
    convs = dout("convs", [DEPTH, 512, NSQ, 3]); lrus = dout("lrus", [DEPTH, 512, NSQ]); pools = dout("pools", [DEPTH, 256, NSQ, 15])

    KTd = nc.dram_tensor("KTd", [DEPTH, 64, 4, SEQ], BF16, kind="Internal").ap()
    Vd = nc.dram_tensor("Vd", [DEPTH, SEQ, 256], BF16, kind="Internal").ap()

    def sb(name, shape, dt=F32):
        return nc.alloc_sbuf_tensor(name, list(shape), dt).ap()

    x = sb("x", [128, 4, D])
    hT = sb("hT", [128, 8, T], BF16)
    mixT = sb("mixT", [128, 8, T], BF16)
    xa = sb("xa", [128, 4, 3 + T])
    ga = sb("ga", [128, 4, T], BF16)
    ub = sb("ub", [128, 2, 15 + T])
    QT = sb("QT", [64, 4, T], BF16)
    KTc = sb("KTc", [64, 4, T], BF16)
    Vc = sb("Vc", [128, 4, 256], BF16)
    qkv = sb("qkv", [128, 768])
    qkb = sb("qkb", [128, 512], BF16)
    wring = [sb(f"wr{i}", [128, 4096], BF16) for i in range(4)]
    wdring = [sb(f"wd{i}", [128, 2, D], BF16) for i in range(2)]
    gw = sb("gw", [128, 2, 4, 128], BF16)
    pw = sb("pw", [128, 2, 128], BF16)
    prms = sb("prms", [128, DEPTH, NPRM])
    cvec = sb("cvec", [128, DEPTH, 4])
    grows = sb("grows", [128, 2, D])
    qkgs = sb("qkgs", [128, 512])
    cst = sb("cst", [128, 768], BF16)
    icnt = sb("icnt", [128, 2, 16])
    TS = [sb(f"ts{i}", [128, 576]) for i in range(10)]
    e_t = sb("e_t", [128, 4, T]); nl_t = sb("nl_t", [128, 4, T], BF16); at_t = sb("at_t", [128, 4, T], BF16); R_t = sb("R_t", [128, 4, T], BF16)
    junk = sb("junk", [128, D]); hn = sb("hn", [128, D], BF16)
    st8 = sb("st8", [128, 16])
    kK = sb("kK", [64, 8192], BF16); kV = sb("kV", [128, 4096], BF16)
    ffT = [sb(f"ffT{i}", [128, 2, T], BF16) for i in range(2)]
    pconv = sb("pconv", [128, DEPTH, 4, 3]); plru = sb("plru", [128, DEPTH, 4]); ppool = sb("ppool", [128, DEPTH, 2, 15])
    sconv_s = sb("sconv_s", [128, 4, NSQ, 3]); slru_s = sb("slru_s", [128, 4, NSQ]); spool_s = sb("spool_s", [128, 2, NSQ, 15])
    hlast = sb("hlast", [128, 4, NSQ])

    zbig = nc.alloc_psum_tensor("zbig", [128, 2048], F32).ap()
    PS = [zbig[:, i * 512:(i + 1) * 512] for i in range(4)] + [nc.alloc_psum_tensor(f"ps{i}", [128, 512], F32).ap() for i in range(4, 8)]

    negtri = cst[:, 0:128]; negones = cst[:, 128:256]; identb = cst[:, 256:384]; tri128 = cst[:, 384:512]
    maskS = [cst[:, 512:576], cst[:, 576:640]]; zerosb = cst[:, 640:768]

    S.dma("pool", cst, cstf, writes=["cst"])
    S.dma("sp", icnt, invcnt, writes=["icnt"])
    S.dma("sp", prms, prm.rearrange("l p n -> p l n"), writes=["prms"])
    for t_, key in ((pconv, "pconv"), (plru, "plru"), (ppool, "ppool")):
        S.op("dve", lambda e, t_=t_: e.memset(t_, 0.0), writes=[key])
    for l in range(DEPTH):
        for k in range(4):
            S.op("act", lambda e, l=l, k=k: e.activation(out=cvec[:, l, k:k + 1], in_=prms[:, l, k * 8 + 7:k * 8 + 8], func=AF.Exp, scale=-1.0),
                 reads=["prms"], writes=["cvec"])
    S.op("act", lambda e: e.activation(out=cvec, in_=cvec, func=AF.Ln, bias=1.0), reads=["cvec"], writes=["cvec"])
    S.op("dve", lambda e: e.tensor_scalar(out=cvec, in0=cvec, scalar1=-8.0, scalar2=None, op0=ALU.mult), reads=["cvec"], writes=["cvec"])

    wr_i = [0]

    def wslot():
        i = wr_i[0] % 4
        wr_i[0] += 1
        return i

    def load_layer_small(l):
        S.dma("pool", gw, gbd[l].rearrange("a k i j -> i a k j"), writes=["gw"])
        S.dma("pool", pw, pbd[l].rearrange("k i j -> i k j"), writes=["pw"])
        S.dma("sp", qkgs, qkg[l], writes=["qkgs"])

    def norm_phase(l, which, NT):
        S.dma("sp", grows[:, which, :], grow[l, which], writes=[f"grows{which}"])
        for i in range(NT):
            S.op("dve", lambda e: e.tensor_tensor(out=junk, in0=x[:, i, :], in1=x[:, i, :], op=ALU.mult), reads=[f"x{i}"], writes=["junk"])
            S.op("dve", lambda e: e.tensor_reduce(out=st8[:, 0:1], in_=junk, axis=AX.X, op=ALU.add), reads=["junk"], writes=["st8"])
            S.op("dve", lambda e: e.tensor_scalar(out=st8[:, 1:2], in0=st8[:, 0:1], scalar1=1.0 / D, scalar2=EPS, op0=ALU.mult, op1=ALU.add), reads=["st8"], writes=["st8"])
            S.op("act", lambda e: e.activation(out=st8[:, 3:4], in_=st8[:, 1:2], func=AF.Sqrt), reads=["st8"], writes=["st8"])
            S.op("dve", lambda e: e.reciprocal(out=st8[:, 2:3], in_=st8[:, 3:4]), reads=["st8"], writes=["st8"])
            S.op("dve", lambda e: e.scalar_tensor_tensor(out=hn, in0=x[:, i, :], scalar=st8[:, 2:3], in1=grows[:, which, :], op0=ALU.mult, op1=ALU.mult),
                 reads=["st8", f"x{i}", f"grows{which}"], writes=["hn"])
            pt = PS[i % 2].bitcast(BF16)
            for k in range(8):
                S.op("pe", lambda e, k=k: e.transpose(out=pt[:, k * 128:(k + 1) * 128], in_=hn[:, k * 128:(k + 1) * 128], identity=identb),
                     reads=["hn", "cst"], writes=[f"P{i % 2}"])
            S.op("act", lambda e: e.activation(out=hT[:, :, i * 128:(i + 1) * 128], in_=pt[:, 0:1024].rearrange("p (k t) -> p k t", k=8), func=AF.Copy),
                 reads=[f"P{i % 2}"], writes=["hT"])

    def load_w(slot, src, key_extra=()):
        S.dma("pool", slot, src, writes=list(key_extra))

    def proj_phase(l, seg):
        NT, N, nseq, L = seg["NT"], seg["N"], seg["nseq"], seg["L"]
        xa_v = xa[:, :, 0:nseq * (3 + L)].rearrange("p k (s c) -> p k s c", s=nseq)
        ub_v = ub[:, :, 0:nseq * (15 + L)].rearrange("p k (s c) -> p k s c", s=nseq)
        if seg["prompt"]:
            S.op("dve", lambda e: e.tensor_copy(out=xa[:, :, 0:3], in_=pconv[:, l, :, :]), reads=["pconv"], writes=["xa"])
            S.op("dve", lambda e: e.tensor_copy(out=ub[:, :, 0:15], in_=ppool[:, l, :, :]), reads=["ppool"], writes=["ub"])
        else:
            S.dma("sp", sconv_s, sconv[l].rearrange("(k p) s j -> p k s j", p=128), writes=["sconv_s"])
            S.dma("sp", slru_s, slru[l].rearrange("(k p) s -> p k s", p=128), writes=["slru_s"])
            S.dma("sp", spool_s, spool[l].rearrange("(k p) s j -> p k s j", p=128), writes=["spool_s"])
            S.op("dve", lambda e: e.tensor_copy(out=xa_v[:, :, :, 0:3], in_=sconv_s), reads=["sconv_s"], writes=["xa"])
            S.op("dve", lambda e: e.tensor_copy(out=ub_v[:, :, :, 0:15], in_=spool_s), reads=["spool_s"], writes=["ub"])
        wv = w_in[l].rearrange("(k p) e -> p k e", p=128)
        for g in range(4):
            si = wslot()
            slot = wring[si].rearrange("p (k e) -> p k e", k=8)
            S.dma("pool", slot, wv[:, :, g * 512:(g + 1) * 512], writes=[f"wr{si}", f"wr{si}b"])
            fm = {0: [0, 1, 2, 3], 1: [0, 1, 2, 3], 2: [0, 1], 3: []}[g]
            for c in fm:
                pb = 2 + (c % 2)
                ps = PS[pb]
                for kc in range(8):
                    S.op("pe", lambda e, kc=kc: e.matmul(ps[:, 0:N], lhsT=slot[:, kc, c * 128:(c + 1) * 128], rhs=hT[:, kc, 0:N], start=(kc == 0), stop=(kc == 7)),
                         reads=[f"wr{si}", "hT"], writes=[f"P{pb}"])
                if g == 0:
                    S.op("act", lambda e: e.activation(out=xa_v[:, c, :, 3:3 + L], in_=ps[:, 0:N].rearrange("p (s t) -> p s t", s=nseq), func=AF.Copy),
                         reads=[f"P{pb}"], writes=["xa"])
                elif g == 1:
                    g_, t_ = TS[0][:, 0:N], TS[1][:, 0:N]
                    S.op("act", lambda e: e.activation(out=g_, in_=ps[:, 0:N], func=AF.Copy), reads=[f"P{pb}"], writes=["ts0"])
                    S.op("dve", lambda e: e.tensor_tensor(out=t_, in0=g_, in1=g_, op=ALU.mult), reads=["ts0"], writes=["ts1"])
                    S.op("dve", lambda e: e.tensor_scalar(out=t_, in0=t_, scalar1=0.044715, scalar2=1.0, op0=ALU.mult, op1=ALU.add), reads=["ts1"], writes=["ts1"])
                    S.op("dve", lambda e: e.tensor_tensor(out=t_, in0=t_, in1=g_, op=ALU.mult), reads=["ts1", "ts0"], writes=["ts1"])
                    S.op("act", lambda e: e.activation(out=t_, in_=t_, func=AF.Sigmoid, scale=1.5957691216057308), reads=["ts1"], writes=["ts1"])
                    S.op("dve", lambda e: e.tensor_tensor(out=ga[:, c, 0:N], in0=t_, in1=g_, op=ALU.mult), reads=["ts1", "ts0"], writes=["ga"])
                else:
                    S.op("act", lambda e: e.activation(out=ub_v[:, c, :, 15:15 + L], in_=ps[:, 0:N].rearrange("p (s t) -> p s t", s=nseq), func=AF.Copy),
                         reads=[f"P{pb}"], writes=["ub"])
            if g == 2:
                slot2 = slot
                si2 = si
        slot3, si3 = slot, si
        import os
        KSUB = int(os.environ.get("KSUB", "9"))
        for i in range(NT if KSUB >= 2 else 0):
            pq, pkv = PS[4], PS[5]
            for kc in range(8):
                S.op("pe", lambda e, kc=kc: e.matmul(pq[:, 0:256], lhsT=hT[:, kc, i * 128:(i + 1) * 128], rhs=slot2[:, kc, 256:512], start=(kc == 0), stop=(kc == 7)),
                     reads=[f"wr{si2}", "hT"], writes=["P4"])
            for kc in range(8):
                S.op("pe", lambda e, kc=kc: e.matmul(pkv[:, 0:512], lhsT=hT[:, kc, i * 128:(i + 1) * 128], rhs=slot3[:, kc, 0:512], start=(kc == 0), stop=(kc == 7)),
                     reads=[f"wr{si3}", "hT"], writes=["P5"])
            S.op("act", lambda e: e.activation(out=qkv[:, 0:256], in_=pq[:, 0:256], func=AF.Copy), reads=["P4"], writes=["qkv"])
            S.op("act", lambda e: e.activation(out=qkv[:, 256:768], in_=pkv[:, 0:512], func=AF.Copy), reads=["P5"], writes=["qkv"])
            if KSUB < 3:
                continue
            sq = TS[2][:, 0:512]
            S.op("dve", lambda e: e.tensor_tensor(out=sq, in0=qkv[:, 0:512], in1=qkv[:, 0:512], op=ALU.mult), reads=["qkv"], writes=["ts2"])
            S.op("dve", lambda e: e.tensor_reduce(out=st8[:, 4:12], in_=sq.rearrange("p (h d) -> p h d", h=8), axis=AX.X, op=ALU.add), reads=["ts2"], writes=["st8"])
            S.op("dve", lambda e: e.tensor_scalar(out=st8[:, 4:12], in0=st8[:, 4:12], scalar1=1.0 / 64, scalar2=EPS, op0=ALU.mult, op1=ALU.add), reads=["st8"], writes=["st8"])
            S.op("act", lambda e: e.activation(out=st8[:, 4:12], in_=st8[:, 4:12], func=AF.Sqrt), reads=["st8"], writes=["st8"])
            S.op("dve", lambda e: e.reciprocal(out=st8[:, 4:12], in_=st8[:, 4:12]), reads=["st8"], writes=["st8"])
            S.op("dve", lambda e: e.tensor_tensor(out=qkv[:, 0:512].rearrange("p (h d) -> p h d", h=8), in0=qkv[:, 0:512].rearrange("p (h d) -> p h d", h=8),
                                                   in1=st8[:, 4:12].unsqueeze(2).to_broadcast([128, 8, 64]), op=ALU.mult), reads=["st8", "qkv"], writes=["qkv"])
            S.op("dve", lambda e: e.tensor_tensor(out=qkv[:, 0:512], in0=qkv[:, 0:512], in1=qkgs, op=ALU.mult), reads=["qkv", "qkgs"], writes=["qkv"])
            if KSUB < 4:
                continue
            r0 = seg["row0"] + i * 128
            kdst, vdst = (kp, vp) if seg["prompt"] else (ksm, vsm)
            S.dma("sp", kdst[l, r0:r0 + 128, :], qkv[:, 256:512], reads=["qkv"])
            S.dma("sp", vdst[l, r0:r0 + 128, :], qkv[:, 512:768], reads=["qkv"])
            S.op("act", lambda e: e.activation(out=qkb[:, 0:256], in_=qkv[:, 0:256], func=AF.Copy, scale=0.125), reads=["qkv"], writes=["qkb"])
            S.op("act", lambda e: e.activation(out=qkb[:, 256:512], in_=qkv[:, 256:512], func=AF.Copy), reads=["qkv"], writes=["qkb"])
            S.op("dve", lambda e: e.tensor_copy(out=Vc[:, i, :], in_=qkv[:, 512:768]), reads=["qkv"], writes=["Vc"])
            if KSUB < 5:
                continue
            for j in range(8):
                pj = PS[6 + j // 4]
                S.op("pe", lambda e, j=j, pj=pj: e.matmul(pj[0:64, (j % 4) * 128:(j % 4 + 1) * 128], lhsT=qkb[:, j * 64:(j + 1) * 64], rhs=identb, start=True, stop=True),
                     reads=["qkb", "cst"], writes=[f"P{6 + j // 4}"])
            S.op("act", lambda e: e.activation(out=QT[:, :, i * 128:(i + 1) * 128], in_=PS[6][0:64, :].rearrange("p (h t) -> p h t", h=4), func=AF.Copy),
                 reads=["P6"], writes=["QT"])
            S.op("dve", lambda e: e.tensor_copy(out=KTc[:, :, i * 128:(i + 1) * 128], in_=PS[7][0:64, :].rearrange("p (h t) -> p h t", h=4)),
                 reads=["P7"], writes=["KTc"])
        if seg["prompt"] and KSUB >= 6:
            j = seg["chunk"]
            S.dma("sp", KTd[l, :, :, j * T:(j + 1) * T], KTc, reads=["KTc"], writes=[f"KTd{l}_{j}"])
            S.dma("sp", Vd[l, j * T:(j + 1) * T, :].rearrange("(b s) f -> s b f", s=128), Vc, reads=["Vc"], writes=[f"Vd{l}_{j}"])

    def lru_phase(l, seg):
        NT, N, nseq, L = seg["NT"], seg["N"], seg["nseq"], seg["L"]
        xa_v = xa[:, :, 0:nseq * (3 + L)].rearrange("p k (s c) -> p k s c", s=nseq)

        def v3(t):
            return t[:, 0:N].rearrange("p (s t) -> p s t", s=nseq)
        for k in range(4):
            pc = lambda j: prms[:, l, k * 8 + j:k * 8 + j + 1]
            xc, xcb = TS[0], TS[1].bitcast(BF16)
            S.op("dve", lambda e: e.tensor_scalar(out=v3(xc), in0=xa_v[:, k, :, 0:L], scalar1=pc(0), scalar2=pc(4), op0=ALU.mult, op1=ALU.add),
                 reads=["xa", "prms"], writes=["ts0"])
            for j in range(1, 4):
                S.op("dve", lambda e, j=j: e.scalar_tensor_tensor(out=v3(xc), in0=xa_v[:, k, :, j:j + L], scalar=pc(j), in1=v3(xc), op0=ALU.mult, op1=ALU.add),
                     reads=["xa", "ts0", "prms"], writes=["ts0"])
            S.op("act", lambda e: e.activation(out=xcb[:, 0:N], in_=xc[:, 0:N], func=AF.Copy), reads=["ts0"], writes=["ts1"])
            S.op("pe", lambda e: e.matmul(PS[2][:, 0:N], lhsT=gw[:, 0, k, :], rhs=xcb[:, 0:N], start=True, stop=True), reads=["gw", "ts1"], writes=["P2"])
            S.op("pe", lambda e: e.matmul(PS[3][:, 0:N], lhsT=gw[:, 1, k, :], rhs=xcb[:, 0:N], start=True, stop=True), reads=["gw", "ts1"], writes=["P3"])
            r_, i_, a_, u_, h_ = TS[2], TS[3], TS[4], TS[5], TS[6]
            S.op("act", lambda e: e.activation(out=r_[:, 0:N], in_=PS[2][:, 0:N], func=AF.Sigmoid, bias=pc(5)), reads=["P2", "prms"], writes=["ts2"])
            S.op("act", lambda e: e.activation(out=i_[:, 0:N], in_=PS[3][:, 0:N], func=AF.Sigmoid, bias=pc(6)), reads=["P3", "prms"], writes=["ts3"])
            S.op("act", lambda e: e.activation(out=a_[:, 0:N], in_=r_[:, 0:N], func=AF.Exp, scale=cvec[:, l, k:k + 1]), reads=["ts2", "cvec"], writes=["ts4"])
            S.op("dve", lambda e: e.tensor_tensor(out=u_[:, 0:N], in0=a_[:, 0:N], in1=a_[:, 0:N], op=ALU.mult), reads=["ts4"], writes=["ts5"])
            S.op("dve", lambda e: e.tensor_scalar(out=u_[:, 0:N], in0=u_[:, 0:N], scalar1=-1.0, scalar2=1.0, op0=ALU.mult, op1=ALU.add), reads=["ts5"], writes=["ts5"])
            S.op("act", lambda e: e.activation(out=u_[:, 0:N], in_=u_[:, 0:N], func=AF.Sqrt), reads=["ts5"], writes=["ts5"])
            S.op("dve", lambda e: e.tensor_tensor(out=i_[:, 0:N], in0=i_[:, 0:N], in1=xc[:, 0:N], op=ALU.mult), reads=["ts3", "ts0"], writes=["ts3"])
            S.op("dve", lambda e: e.tensor_tensor(out=u_[:, 0:N], in0=u_[:, 0:N], in1=i_[:, 0:N], op=ALU.mult), reads=["ts5", "ts3"], writes=["ts5"])
            for s in range(nseq):
                init = plru[:, l, k:k + 1] if seg["prompt"] else slru_s[:, k, s:s + 1]
                S.op("dve", lambda e, s=s, init=init: e.tensor_tensor_scan(out=h_[:, s * L:(s + 1) * L], data0=a_[:, s * L:(s + 1) * L], data1=u_[:, s * L:(s + 1) * L],
                                                                           initial=init, op0=ALU.mult, op1=ALU.add),
                     reads=["ts4", "ts5", "plru", "slru_s"], writes=["ts6"])
            if seg["prompt"]:
                S.op("dve", lambda e: e.tensor_copy(out=plru[:, l, k:k + 1], in_=h_[:, L - 1:L]), reads=["ts6"], writes=["plru"])
            else:
                S.op("dve", lambda e: e.tensor_copy(out=hlast[:, k, :], in_=h_[:, L - 1:N:L]), reads=["ts6"], writes=["hlast"])
            S.op("dve", lambda e: e.tensor_tensor(out=mixT[:, k, 0:N], in0=h_[:, 0:N], in1=ga[:, k, 0:N], op=ALU.mult), reads=["ts6", "ga"], writes=["mixT"])
        if seg["prompt"]:
            S.op("dve", lambda e: e.tensor_copy(out=pconv[:, l, :, :], in_=xa[:, :, L:L + 3]), reads=["xa"], writes=["pconv"])
        else:
            for k in range(4):
                S.dma("sp", convs[l, k * 128:(k + 1) * 128], xa_v[:, k, :, L:L + 3], reads=["xa"])
            S.dma("sp", lrus[l].rearrange("(k p) s -> p k s", p=128), hlast, reads=["hlast"])

    def pool_phase(l, seg):
        NT, N, nseq, L = seg["NT"], seg["N"], seg["nseq"], seg["L"]
        W_ = nseq * (15 + L)
        ub_v = ub[:, :, 0:W_].rearrange("p k (s c) -> p k s c", s=nseq)
        for k in range(2):
            pa = TS[7][:, 0:W_].rearrange("p (s c) -> p s c", s=nseq)
            pb = TS[8][:, 0:W_].rearrange("p (s c) -> p s c", s=nseq)
            C = 15 + L
            u_ = ub_v[:, k]
            S.op("dve", lambda e: e.tensor_tensor(out=pa[:, :, 1:C], in0=u_[:, :, 1:C], in1=u_[:, :, 0:C - 1], op=ALU.add), reads=["ub"], writes=["ts7"])
            S.op("dve", lambda e: e.tensor_tensor(out=pb[:, :, 3:C], in0=pa[:, :, 3:C], in1=pa[:, :, 1:C - 2], op=ALU.add), reads=["ts7"], writes=["ts8"])
            if k == 1:
                S.op("dve", lambda e: e.tensor_tensor(out=pa[:, :, 7:C], in0=pb[:, :, 7:C], in1=pb[:, :, 3:C - 4], op=ALU.add), reads=["ts8"], writes=["ts7"])
                S.op("dve", lambda e: e.tensor_tensor(out=pb[:, :, 15:C], in0=pa[:, :, 15:C], in1=pa[:, :, 7:C - 8], op=ALU.add), reads=["ts7"], writes=["ts8"])
            ws = (2, 4) if k == 0 else (8, 16)
            d_ = TS[9].bitcast(BF16)[:, 0:N].rearrange("p (s t) -> p s t", s=nseq)
            for half, (src, w) in enumerate(((pa, ws[0]), (pb, ws[1]))):
                p0, p1 = half * 64, half * 64 + 64
                S.op("dve", lambda e, src=src, w=w, p0=p0, p1=p1: e.scalar_tensor_tensor(out=d_[p0:p1], in0=src[p0:p1, :, 15:C], scalar=1.0 / w, in1=u_[p0:p1, :, 15:C],
                                                                                       op0=ALU.mult, op1=ALU.subtract), reads=["ts7", "ts8", "ub"], writes=["ts9"])
                if seg["prompt"] and seg["chunk"] == 0:
                    tmp = st8[p0:p1, 0:16]
                    S.op("dve", lambda e, src=src, p0=p0, p1=p1, tmp=tmp: e.tensor_tensor(out=tmp, in0=src[p0:p1, 0, 15:31], in1=icnt[p0:p1, k, :], op=ALU.mult),
                         reads=["ts7", "ts8", "icnt"], writes=["st8"])
                    S.op("dve", lambda e, p0=p0, p1=p1, tmp=tmp: e.tensor_tensor(out=d_[p0:p1, 0, 0:16], in0=tmp, in1=u_[p0:p1, 0, 15:31], op=ALU.subtract),
                         reads=["st8", "ub"], writes=["ts9"])
            S.op("pe", lambda e: e.matmul(PS[2][:, 0:N], lhsT=pw[:, k, :], rhs=TS[9].bitcast(BF16)[:, 0:N], start=True, stop=True), reads=["pw", "ts9"], writes=["P2"])
            S.op("act", lambda e: e.activation(out=mixT[:, 4 + k, 0:N], in_=PS[2][:, 0:N], func=AF.Copy, scale=prms[:, l, 32 + k:33 + k]), reads=["P2", "prms"], writes=["mixT"])
        if seg["prompt"]:
            S.op("dve", lambda e: e.tensor_copy(out=ppool[:, l, :, :], in_=ub[:, :, L:L + 15]), reads=["ub"], writes=["ppool"])
        else:
            for k in range(2):
                S.dma("sp", pools[l, k * 128:(k + 1) * 128], ub_v[:, k, :, L:L + 15], reads=["ub"])

    zps = [PS[0], PS[1], PS[2], PS[3]]
    ops_ = [PS[4], PS[5], PS[6], PS[7]]
    zkeys = ["P0", "P1", "P2", "P3"]
    okeys = ["P4", "P5", "P6", "P7"]

    def att_unit(KT, KTkeys, V, Vkeys, nk, c0, c1, mask, first, last):
        w = c1 - c0
        for h in range(4):
            S.op("pe", lambda e, h=h: e.matmul(zps[h][0:nk, c0:c1], lhsT=KT[:, h, :], rhs=QT[:, h, c0:c1], start=True, stop=False),
                 reads=KTkeys + ["QT"], writes=[zkeys[h]])
        zall = zbig[0:nk, :].rearrange("p (h t) -> p h t", h=4)[:, :, c0:c1]
        S.op("act", lambda e: e.activation(out=e_t[0:nk, :, c0:c1], in_=zall, func=AF.Exp), reads=zkeys, writes=["e0", "e1", "e2", "e3"])
        S.op("act", lambda e: e.activation(out=nl_t[0:nk, :, c0:c1], in_=e_t[0:nk, :, c0:c1], func=AF.Ln, bias=1.0), reads=["e0", "e1", "e2", "e3"], writes=["nl"])
        if mask is not None:
            mw = mask.shape[1]
            S.op("dve", lambda e: e.tensor_tensor(out=nl_t[0:nk, :, c0:c0 + mw], in0=nl_t[0:nk, :, c0:c0 + mw],
                                                   in1=mask.unsqueeze(1).to_broadcast([nk, 4, mw]), op=ALU.mult), reads=["nl", "cst"], writes=["nl"])
        for h in range(4):
            S.op("pe", lambda e, h=h: e.matmul(zps[h][0:nk, c0:c1], lhsT=negtri[0:nk, 0:nk], rhs=nl_t[0:nk, h, c0:c1], start=False, stop=first),
                 reads=["nl", "cst"], writes=[zkeys[h]])
            if not first:
                S.op("pe", lambda e, h=h: e.matmul(zps[h][0:nk, c0:c1], lhsT=negones[:, 0:nk], rhs=R_t[:, h, c0:c1], start=False, stop=True),
                     reads=["R", "cst"], writes=[zkeys[h]])
        S.op("act", lambda e: e.activation(out=at_t[0:nk, :, c0:c1], in_=zall, func=AF.Exp), reads=zkeys, writes=["at0", "at1", "at2", "at3"])
        if mask is not None:
            mw = mask.shape[1]
            S.op("dve", lambda e: e.tensor_tensor(out=at_t[0:nk, :, c0:c0 + mw], in0=at_t[0:nk, :, c0:c0 + mw],
                                                   in1=mask.unsqueeze(1).to_broadcast([nk, 4, mw]), op=ALU.mult), reads=["at0", "at1", "at2", "at3", "cst"],
                 writes=["at0", "at1", "at2", "at3"])
        if not last:
            S.op("dve", lambda e: e.tensor_tensor(out=R_t[0:nk, :, c0:c1], in0=R_t[0:nk, :, c0:c1], in1=nl_t[0:nk, :, c0:c1], op=ALU.add), reads=["nl", "R"], writes=["R"])
        for h in range(4):
            if h % 2 == 0:
                S.op("pe", lambda e, h=h: e.matmul(ops_[h][0:64, c0:c1], lhsT=V[0:nk, h * 64:(h + 1) * 64], rhs=at_t[0:nk, h, c0:c1], start=False, stop=last),
                     reads=Vkeys + [f"at{h}"], writes=[okeys[h]])
            else:
                S.op("pe", lambda e, h=h: e.matmul(ops_[h][0:128, c0:c1], lhsT=V[0:nk, (h - 1) * 64:(h + 1) * 64], rhs=at_t[0:nk, h, c0:c1], start=False, stop=last),
                     reads=Vkeys + [f"at{h}"], writes=[okeys[h]])

    def att_begin(c0, c1):
        S.op("dve", lambda e: e.memset(R_t[:, :, c0:c1], 0.0), writes=["R"])
        for h in range(4):
            S.op("pe", lambda e, h=h: e.matmul(ops_[h][:, c0:c1], lhsT=zerosb, rhs=hT[:, 0, c0:c1], start=True, stop=False), reads=["cst", "hT"], writes=[okeys[h]])

    def att_end(c0, c1):
        for h in range(4):
            p0 = (h % 2) * 64
            S.op("act", lambda e, h=h, p0=p0: e.activation(out=mixT[p0:p0 + 64, 6 + h // 2, c0:c1], in_=ops_[h][p0:p0 + 64, c0:c1], func=AF.Copy),
                 reads=[okeys[h]], writes=["mixT"])

    def attn_prompt(l, j):
        att_begin(0, T)
        nhist = 4 * j
        units = []
        for jj in (3, 2, 1, 0):
            units.append(("cur", jj))
        npieces = (nhist + 7) // 8
        for p in range(npieces - 1, -1, -1):
            nb = min(8, nhist - 8 * p)
            for b in range(nb - 1, -1, -1):
                units.append(("hist", p, b, nb))
        loaded = {}
        pslot = [0]
        for ui, u in enumerate(units):
            first, last = ui == 0, ui == len(units) - 1
            if u[0] == "cur":
                jj = u[1]
                att_unit(KTc[:, :, jj * 128:(jj + 1) * 128], ["KTc"], Vc[:, jj, :], ["Vc"], 128, jj * 128, T, tri128, first, last)
            else:
                _, p, b, nb = u
                if p not in loaded:
                    s = pslot[0] % 2
                    pslot[0] += 1
                    Kv = kK[:, s * 4096:(s + 1) * 4096].rearrange("p (h t) -> p h t", h=4)
                    Vv = kV[:, s * 2048:(s + 1) * 2048].rearrange("p (b f) -> p b f", b=8)
                    t0 = p * 1024
                    rk = [f"KTd{l}_{c}" for c in range(2 * p, min(2 * p + 2, j))]
                    rv = [f"Vd{l}_{c}" for c in range(2 * p, min(2 * p + 2, j))]
                    S.dma("sp", Kv[:, :, 0:nb * 128], KTd[l, :, :, t0:t0 + nb * 128], reads=rk, writes=[f"kvK{s}"])
                    S.dma("sp", Vv[:, 0:nb, :], Vd[l, t0:t0 + nb * 128, :].rearrange("(b s) f -> s b f", s=128), reads=rv, writes=[f"kvV{s}"])
                    loaded[p] = (Kv, Vv, s)
                Kv, Vv, s = loaded[p]
                att_unit(Kv[:, :, b * 128:(b + 1) * 128], [f"kvK{s}"], Vv[:, b, :], [f"kvV{s}"], 128, 0, T, None, first, last)
        att_end(0, T)

    def attn_sample(l):
        Kv = kK.rearrange("p (h t) -> p h t", h=4)
        Vv = kV.rearrange("p (b f) -> p b f", b=16)
        for s in range(NSQ):
            c0, c1 = s * LS, (s + 1) * LS
            S.dma("pool", Kv[:, :, 0:PAST], ckT[l, s], writes=["kvK0", "kvK1"])
            S.dma("pool", Vv[:, 0:NPB, :], cv[l, s].rearrange("(b s) f -> s b f", s=128), writes=["kvV0", "kvV1"])
            att_begin(c0, c1)
            tl = s // 2
            att_unit(KTc[:, :, tl * 128:(tl + 1) * 128], ["KTc"], Vc[:, tl, :], ["Vc"], 128, c0, c1, maskS[s % 2], True, False)
            for b in range(NPB - 1, -1, -1):
                att_unit(Kv[:, :, b * 128:(b + 1) * 128], ["kvK0", "kvK1"], Vv[:, b, :], ["kvV0", "kvV1"], 128, c0, c1, None, False, b == 0)
            att_end(c0, c1)

    def wout_phase(l, seg):
        NT = seg["NT"]
        wv = w_out[l].rearrange("(k p) e -> p k e", p=128)
        for dh in range(2):
            si = wslot()
            slot = wring[si].rearrange("p (k e) -> p k e", k=8)
            S.dma("pool", slot, wv[:, :, dh * 512:(dh + 1) * 512], writes=[f"wr{si}", f"wr{si}b"])
            for i in range(NT):
                pb = i % 2
                for kc in range(8):
                    S.op("pe", lambda e, kc=kc: e.matmul(PS[pb], lhsT=mixT[:, kc, i * 128:(i + 1) * 128], rhs=slot[:, kc, :], start=(kc == 0), stop=(kc == 7)),
                         reads=["mixT", f"wr{si}"], writes=[f"P{pb}"])
                S.op("dve", lambda e: e.tensor_tensor(out=x[:, i, dh * 512:(dh + 1) * 512], in0=x[:, i, dh * 512:(dh + 1) * 512], in1=PS[pb], op=ALU.add),
                     reads=[f"P{pb}", f"x{i}"], writes=[f"x{i}"])

    def ffn_phase(l, seg):
        NT, N = seg["NT"], seg["N"]
        wg = w_gate[l].rearrange("(k p) f -> p k f", p=128)
        wu = w_up[l].rearrange("(k p) f -> p k f", p=128)
        wd = w_down[l].rearrange("(c p) d -> p c d", p=128)
        for fg in range(DFF // 256):
            si = wslot()
            slot = wring[si].rearrange("p (a k f) -> p a k f", a=2, k=8)
            S.dma("pool", slot[:, 0], wg[:, :, fg * 256:(fg + 1) * 256], writes=[f"wr{si}", f"wr{si}b"])
            S.dma("pool", slot[:, 1], wu[:, :, fg * 256:(fg + 1) * 256], writes=[f"wr{si}b"])
            di = fg % 2
            S.dma("pool", wdring[di], wd[:, fg * 2:fg * 2 + 2, :], writes=[f"wd{di}"])
            ff = ffT[fg % 2]
            for fc in range(2):
                pg, pu = PS[2 + fc * 2], PS[3 + fc * 2]
                for kc in range(8):
                    S.op("pe", lambda e, kc=kc: e.matmul(pg[:, 0:N], lhsT=slot[:, 0, kc, fc * 128:(fc + 1) * 128], rhs=hT[:, kc, 0:N], start=(kc == 0), stop=(kc == 7)),
                         reads=[f"wr{si}", "hT"], writes=[f"P{2 + fc * 2}"])
                for kc in range(8):
                    S.op("pe", lambda e, kc=kc: e.matmul(pu[:, 0:N], lhsT=slot[:, 1, kc, fc * 128:(fc + 1) * 128], rhs=hT[:, kc, 0:N], start=(kc == 0), stop=(kc == 7)),
                         reads=[f"wr{si}b", "hT"], writes=[f"P{3 + fc * 2}"])
                sg = TS[fc][:, 0:N]
                S.op("act", lambda e: e.activation(out=sg, in_=pg[:, 0:N], func=AF.Silu), reads=[f"P{2 + fc * 2}"], writes=[f"ts{fc}"])
                S.op("dve", lambda e: e.tensor_tensor(out=ff[:, fc, 0:N], in0=sg, in1=pu[:, 0:N], op=ALU.mult), reads=[f"ts{fc}", f"P{3 + fc * 2}"], writes=[f"ff{fg % 2}"])
            for i in range(NT):
                for dh in range(2):
                    pb = (i * 2 + dh) % 2
                    for fc in range(2):
                        S.op("pe", lambda e, fc=fc: e.matmul(PS[pb], lhsT=ff[:, fc, i * 128:(i + 1) * 128], rhs=wdring[di][:, fc, dh * 512:(dh + 1) * 512], start=(fc == 0), stop=(fc == 1)),
                             reads=[f"ff{fg % 2}", f"wd{di}"], writes=[f"P{pb}"])
                    S.op("dve", lambda e: e.tensor_tensor(out=x[:, i, dh * 512:(dh + 1) * 512], in0=x[:, i, dh * 512:(dh + 1) * 512], in1=PS[pb], op=ALU.add),
                         reads=[f"P{pb}", f"x{i}"], writes=[f"x{i}"])

    def run_segment(seg):
        NT = seg["NT"]
        src = xp if seg["prompt"] else xs
        r0 = seg["row0"]
        for i in range(NT):
            S.dma("sp", x[:, i, :], src[r0 + i * 128:r0 + (i + 1) * 128, :], writes=[f"x{i}"])
        import os
        KS = int(os.environ.get("KSTOP", "9"))
        for l in range(DEPTH):
            load_layer_small(l)
            norm_phase(l, 0, NT)
            if KS >= 2:
                proj_phase(l, seg)
            if KS >= 3:
                lru_phase(l, seg)
            if KS >= 4:
                pool_phase(l, seg)
            if KS >= 5:
                if seg["prompt"]:
                    attn_prompt(l, seg["chunk"])
                else:
                    attn_sample(l)
            if KS >= 6:
                wout_phase(l, seg)
            if KS >= 7:
                norm_phase(l, 1, NT)
                ffn_phase(l, seg)
        dst = yp if seg["prompt"] else ys
        for i in range(NT):
            S.dma("sp", dst[r0 + i * 128:r0 + (i + 1) * 128, :], x[:, i, :], reads=[f"x{i}"])

    for j in range(NCH):
        run_segment(dict(prompt=True, chunk=j, NT=4, N=T, nseq=1, L=T, row0=j * T))
    for l in range(DEPTH):
        S.dma("sp", convp[l].rearrange("(k p) j -> p k j", p=128), pconv[:, l, :, :], reads=["pconv"])
        S.dma("sp", lrup[l].rearrange("(k p) o -> p k o", p=128), plru[:, l, :].unsqueeze(2), reads=["plru"], allow_slow_non_contiguous=True)
        S.dma("sp", poolp[l].rearrange("(k p) j -> p k j", p=128), ppool[:, l, :, :], reads=["ppool"])
    import os
    if os.environ.get("KSAMPLE", "1") == "1":
        run_segment(dict(prompt=False, chunk=0, NT=2, N=NSQ * LS, nseq=NSQ, L=LS, row0=0))
    S.finish()
    return nc, S


def _consts():
    c = np.zeros((128, 768), np.float32)
    j = np.arange(128)[:, None]; s = np.arange(128)[None, :]
    c[:, 0:128] = -(j >= s).astype(np.float32)
    c[:, 128:256] = -1.0
    c[:, 256:384] = np.eye(128, dtype=np.float32)
    c[:, 384:512] = (j < s).astype(np.float32)
    t64 = (np.arange(64)[:, None] < np.arange(64)[None, :]).astype(np.float32)
    c[0:64, 512:576] = t64
    c[64:128, 576:640] = t64
    return c


def _invcnt():
    ic = np.zeros((128, 2, 16), np.float32)
    ws = {(0, 0): 2, (0, 1): 4, (1, 0): 8, (1, 1): 16}
    pos = np.arange(16)
    for (k, half), w in ws.items():
        ic[half * 64:(half + 1) * 64, k, :] = 1.0 / np.minimum(pos + 1, w)
    return ic


def _blockdiag(w):
    Ld, nb = w.shape[0], w.shape[1]
    out = np.zeros((Ld, nb // 2, 128, 128), np.float32)
    for b in range(nb):
        k, h = b // 2, b % 2
        out[:, k, h * 64:(h + 1) * 64, h * 64:(h + 1) * 64] = w[:, b]
    return out


_CACHE = {}


def kernel(x_prompt, x_sample, cache_k, cache_v, state_conv, state_lru, state_pool,
           attn_norm, w_in, conv_w, conv_b, gate_a_w, gate_a_b, gate_x_w, gate_x_b,
           lru_lambda, pool_w, pool_scale, q_norm, k_norm, w_out, ffn_norm,
           w_gate, w_up, w_down):
    cfg = dict(CFG)
    SEQ, DEPTH, PAST = cfg["SEQ"], cfg["DEPTH"], cfg["PAST"]
    f = lambda a: np.ascontiguousarray(np.asarray(a, dtype=np.float32))
    key = (SEQ, DEPTH, PAST)
    if key not in _CACHE:
        _CACHE[key] = build_program(cfg)
    nc, S = _CACHE[key]
    prm = np.zeros((DEPTH, 128, NPRM), np.float32)
    cw = f(conv_w); cb = f(conv_b); gab = f(gate_a_b).reshape(DEPTH, 512); gxb = f(gate_x_b).reshape(DEPTH, 512); lam = f(lru_lambda); psc = f(pool_scale)
    for k in range(4):
        sl = slice(k * 128, (k + 1) * 128)
        for j in range(4):
            prm[:, :, k * 8 + j] = cw[:, j, sl]
        prm[:, :, k * 8 + 4] = cb[:, sl]
        prm[:, :, k * 8 + 5] = gab[:, sl]
        prm[:, :, k * 8 + 6] = gxb[:, sl]
        prm[:, :, k * 8 + 7] = lam[:, sl]
    for k in range(2):
        prm[:, :, 32 + k] = psc[:, k * 128:(k + 1) * 128]
    grow = np.ascontiguousarray(np.broadcast_to(np.stack([f(attn_norm), f(ffn_norm)], 1)[:, :, None, :], (DEPTH, 2, 128, D)))
    qkg = np.concatenate([np.tile(f(q_norm), (1, 4)), np.tile(f(k_norm), (1, 4))], 1)
    qkg = np.ascontiguousarray(np.broadcast_to(qkg[:, None, :], (DEPTH, 128, 512)))
    gbd = np.ascontiguousarray(np.stack([_blockdiag(f(gate_a_w)), _blockdiag(f(gate_x_w))], 1))
    pbd = _blockdiag(f(pool_w))
    common = dict(xp=f(x_prompt)[0], w_in=f(w_in), w_out=f(w_out), w_gate=f(w_gate), w_up=f(w_up), w_down=f(w_down),
                  gbd=gbd, pbd=pbd, prm=prm, grow=grow, qkg=qkg, cstf=_consts(), invcnt=_invcnt())
    ck = f(cache_k); cvv = f(cache_v); sc = f(state_conv); sl_ = f(state_lru); spl = f(state_pool); xsm = f(x_sample)
    in_maps = []
    for c in range(8):
        b0, b1 = c * NSQ, (c + 1) * NSQ
        m = dict(common)
        m["xs"] = np.ascontiguousarray(xsm[b0:b1].reshape(NSQ * LS, D))
        m["ckT"] = np.ascontiguousarray(ck[:, b0:b1].transpose(0, 1, 4, 3, 2))
        m["cv"] = np.ascontiguousarray(cvv[:, b0:b1].reshape(DEPTH, NSQ, PAST, 256))
        m["sconv"] = np.ascontiguousarray(sc[:, b0:b1].transpose(0, 3, 1, 2))
        m["slru"] = np.ascontiguousarray(sl_[:, b0:b1].transpose(0, 2, 1))
        m["spool"] = np.ascontiguousarray(spl[:, b0:b1].transpose(0, 3, 1, 2))
        in_maps.append(m)
    res = run_bass_kernel_spmd(nc, in_maps, core_ids=list(range(8)))
    R = res.results
    r0 = R[0]
    y_prompt = r0["yp"].reshape(1, SEQ, D)
    nkp = r0["kp"].reshape(DEPTH, 1, SEQ, 4, 64); nvp = r0["vp"].reshape(DEPTH, 1, SEQ, 4, 64)
    ncp = r0["convp"].transpose(0, 2, 1).reshape(DEPTH, 1, 3, 512)
    nhp = r0["lrup"].reshape(DEPTH, 1, 512)
    npp = r0["poolp"].transpose(0, 2, 1).reshape(DEPTH, 1, 15, 256)
    y_sample = np.concatenate([R[c]["ys"].reshape(NSQ, LS, D) for c in range(8)], 0)
    nks = np.concatenate([R[c]["ksm"].reshape(DEPTH, NSQ, LS, 4, 64) for c in range(8)], 1)
    nvs = np.concatenate([R[c]["vsm"].reshape(DEPTH, NSQ, LS, 4, 64) for c in range(8)], 1)
    ncs = np.concatenate([R[c]["convs"].transpose(0, 2, 3, 1) for c in range(8)], 1)
    nhs = np.concatenate([R[c]["lrus"].transpose(0, 2, 1) for c in range(8)], 1)
    nps = np.concatenate([R[c]["pools"].transpose(0, 2, 3, 1) for c in range(8)], 1)
    outs = (y_prompt, y_sample, nkp, nvp, ncp, nhp, npp, nks, nvs, ncs, nhs, nps)
    return tuple(np.ascontiguousarray(o, dtype=np.float32) for o in outs)
```

```python
import numpy as np
import concourse.bass as bass
import concourse.mybir as mybir
from concourse.bass_utils import run_bass_kernel_spmd

F32 = mybir.dt.float32
BF16 = mybir.dt.bfloat16
AF = mybir.ActivationFunctionType
ALU = mybir.AluOpType
AX = mybir.AxisListType

CFG = {"SEQ": 16384, "DEPTH": 4, "PAST": 2048}
D = 1024
DIN = 2048
DFF = 2816
T = 512
NSQ = 4
LS = 64
EPS = 1e-6
NPRM = 34
NDMA = 6


class Sched:
    def __init__(self, nc):
        self.nc = nc
        self.eng = {"pe": nc.tensor, "act": nc.scalar, "dve": nc.vector, "pool": nc.gpsimd, "sp": nc.sync}
        self.sem = {e: nc.alloc_semaphore(f"sem_{e}") for e in self.eng}
        self.cnt = {e: 0 for e in self.eng}
        self.seen = {e: {} for e in self.eng}
        self.last_w = {}
        self.readers = {}
        self.dsem = {q: [[nc.alloc_semaphore(f"dsem_{q}{i}"), 0] for i in range(NDMA)] for q in ("sp", "pool")}
        self.dnext = {"sp": 0, "pool": 0}
        self.ninstr = 0

    def _wait(self, e, deps, extra=None):
        eng = self.eng[e]
        best = {}
        for h in deps:
            if h is None:
                continue
            key, val = h
            if e == "pe" and key == "pe":
                continue
            if best.get(key, 0) < val:
                best[key] = val
        need = []
        for key, val in best.items():
            if self.seen[e].get(key, 0) >= val:
                continue
            sem = self.sem[key] if isinstance(key, str) else self.dsem[key[0]][key[1]][0]
            need.append((sem, val))
            self.seen[e][key] = val
        if extra is not None:
            need.append(extra)
        for sem, val in need[:-1]:
            eng.wait_ge(sem, val)
            self.ninstr += 1
        return need[-1] if need else None

    def _deps(self, reads, writes):
        deps = []
        for k in reads:
            deps.append(self.last_w.get(k))
        for k in writes:
            deps.append(self.last_w.get(k))
            deps.extend(self.readers.get(k, {}).values())
        return deps

    def _record(self, h, reads, writes):
        for k in writes:
            self.last_w[k] = h
            self.readers[k] = {}
        for k in reads:
            self.readers.setdefault(k, {})[h[0]] = h

    def op(self, e, emit, reads=(), writes=()):
        emb = self._wait(e, self._deps(reads, writes))
        ins = emit(self.eng[e])
        if emb is not None:
            ins._wait_ge(emb[0], emb[1])
        self.cnt[e] += 1
        ins.then_inc(self.sem[e], 1)
        self.ninstr += 1
        h = (e, self.cnt[e])
        self._record(h, reads, writes)
        return h

    def dma(self, q, out, in_, reads=(), writes=(), **kw):
        i = self.dnext[q] % NDMA
        self.dnext[q] += 1
        slot = self.dsem[q][i]
        extra = None
        if slot[1] > 0 and self.seen[q].get((q, i), 0) < slot[1]:
            extra = (slot[0], slot[1])
            self.seen[q][(q, i)] = slot[1]
        emb = self._wait(q, self._deps(reads, writes), extra)
        if emb is not None:
            self.eng[q].wait_ge(emb[0], emb[1])
            self.ninstr += 1
        ins = self.eng[q].dma_start(out=out, in_=in_, **kw)
        slot[1] += 16
        ins.then_inc(slot[0], 16)
        self.ninstr += 1
        h = ((q, i), slot[1])
        self._record(h, reads, writes)
        return h

    def finish(self):
        for q in ("sp", "pool"):
            for i, slot in enumerate(self.dsem[q]):
                if slot[1] > 0:
                    self.eng[q].wait_ge(slot[0], slot[1])


def build_program(cfg):
    SEQ, DEPTH, PAST = cfg["SEQ"], cfg["DEPTH"], cfg["PAST"]
    NCH = SEQ // T
    NPB = PAST // 128
    nc = bass.Bass("TRN2", target_bir_lowering=False)
    S = Sched(nc)

    def din(name, shape, dt=F32):
        return nc.dram_tensor(name, list(shape), dt, kind="ExternalInput").ap()

    def dout(name, shape, dt=F32):
        return nc.dram_tensor(name, list(shape), dt, kind="ExternalOutput").ap()

    xp = din("xp", [SEQ, D]); xs = din("xs", [NSQ * LS, D])
    ckT = din("ckT", [DEPTH, NSQ, 64, 4, PAST]); cv = din("cv", [DEPTH, NSQ, PAST, 256])
    sconv = din("sconv", [DEPTH, 512, NSQ, 3]); slru = din("slru", [DEPTH, 512, NSQ]); spool = din("spool", [DEPTH, 256, NSQ, 15])
    w_in = din("w_in", [DEPTH, D, DIN]); w_out = din("w_out", [DEPTH, D, D])
    w_gate = din("w_gate", [DEPTH, D, DFF]); w_up = din("w_up", [DEPTH, D, DFF]); w_down = din("w_down", [DEPTH, DFF, D])
    gbd = din("gbd", [DEPTH, 2, 4, 128, 128]); pbd = din("pbd", [DEPTH, 2, 128, 128])
    prm = din("prm", [DEPTH, 128, NPRM]); grow = din("grow", [DEPTH, 2, 128, D]); qkg = din("qkg", [DEPTH, 128, 512])
    cstf = din("cstf", [128, 768]); invcnt = din("invcnt", [128, 2, 16])

    yp = dout("yp", [SEQ, D]); ys = dout("ys", [NSQ * LS, D])
    kp = dout("kp", [DEPTH, SEQ, 256]); vp = dout("vp", [DEPTH, SEQ, 256])
    convp = dout("convp", [DEPTH, 512, 3]); lrup = dout("lrup", [DEPTH, 512, 1]); poolp = dout("poolp", [DEPTH, 256, 15])
    ksm = dout("ksm", [DEPTH, NSQ * LS, 256]); vsm = dout("vsm", [DEPTH, NSQ * LS, 256])
    convs = dout("convs", [DEPTH, 512, NSQ, 3]); lrus = dout("lrus", [DEPTH, 512, NSQ]); pools = dout("pools", [DEPTH, 256, NSQ, 15])

    KTd = nc.dram_tensor("KTd", [DEPTH, 64, 4, SEQ], BF16, kind="Internal").ap()
    Vd = nc.dram_tensor("Vd", [DEPTH, SEQ, 256], BF16, kind="Internal").ap()

    def sb(name, shape, dt=F32):
        return nc.alloc_sbuf_tensor(name, list(shape), dt).ap()

    x = sb("x", [128, 4, D])
    hT = sb("hT", [128, 8, T], BF16)
    mixT = sb("mixT", [128, 8, T], BF16)
    xa = sb("xa", [128, 4, 3 + T])
    ga = sb("ga", [128, 4, T], BF16)
    ub = sb("ub", [128, 2, 15 + T])
    QT = sb("QT", [64, 4, T], BF16)
    KTc = sb("KTc", [64, 4, T], BF16)
    Vc = sb("Vc", [128, 4, 256], BF16)
    qkv = sb("qkv", [128, 768])
    qkb = sb("qkb", [128, 512], BF16)
    wring = [sb(f"wr{i}", [128, 4096], BF16) for i in range(4)]
    wdring = [sb(f"wd{i}", [128, 2, D], BF16) for i in range(2)]
    gw = sb("gw", [128, 2, 4, 128], BF16)
    pw = sb("pw", [128, 2, 128], BF16)
    prms = sb("prms", [128, DEPTH, NPRM])
    cvec = sb("cvec", [128, DEPTH, 4])
    grows = sb("grows", [128, 2, D])
    qkgs = sb("qkgs", [128, 512])
    cst = sb("cst", [128, 768], BF16)
    icnt = sb("icnt", [128, 2, 16])
    TS = [sb(f"ts{i}", [128, 576]) for i in range(10)]
    e_t = sb("e_t", [128, 4, T]); nl_t = sb("nl_t", [128, 4, T], BF16); at_t = sb("at_t", [128, 4, T], BF16); R_t = sb("R_t", [128, 4, T], BF16)
    junk = sb("junk", [128, D]); hn = sb("hn", [128, D], BF16)
    st8 = sb("st8", [128, 16])
    kK = sb("kK", [64, 8192], BF16); kV = sb("kV", [128, 4096], BF16)
    ffT = [sb(f"ffT{i}", [128, 2, T], BF16) for i in range(2)]
    pconv = sb("pconv", [128, DEPTH, 4, 3]); plru = sb("plru", [128, DEPTH, 4]); ppool = sb("ppool", [128, DEPTH, 2, 15])
    sconv_s = sb("sconv_s", [128, 4, NSQ, 3]); slru_s = sb("slru_s", [128, 4, NSQ]); spool_s = sb("spool_s", [128, 2, NSQ, 15])
    hlast = sb("hlast", [128, 4, NSQ])

    zbig = nc.alloc_psum_tensor("zbig", [128, 2048], F32).ap()
    PS = [zbig[:, i * 512:(i + 1) * 512] for i in range(4)] + [nc.alloc_psum_tensor(f"ps{i}", [128, 512], F32).ap() for i in range(4, 8)]

    negtri = cst[:, 0:128]; negones = cst[:, 128:256]; identb = cst[:, 256:384]; tri128 = cst[:, 384:512]
    maskS = [cst[:, 512:576], cst[:, 576:640]]; zerosb = cst[:, 640:768]

    S.dma("pool", cst, cstf, writes=["cst"])
    S.dma("sp", icnt, invcnt, writes=["icnt"])
    S.dma("sp", prms, prm.rearrange("l p n -> p l n"), writes=["prms"])
    for t_, key in ((pconv, "pconv"), (plru, "plru"), (ppool, "ppool")):
        S.op("dve", lambda e, t_=t_: e.memset(t_, 0.0), writes=[key])
    for l in range(DEPTH):
        for k in range(4):
            S.op("act", lambda e, l=l, k=k: e.activation(out=cvec[:, l, k:k + 1], in_=prms[:, l, k * 8 + 7:k * 8 + 8], func=AF.Exp, scale=-1.0),
                 reads=["prms"], writes=["cvec"])
    S.op("act", lambda e: e.activation(out=cvec, in_=cvec, func=AF.Ln, bias=1.0), reads=["cvec"], writes=["cvec"])
    S.op("dve", lambda e: e.tensor_scalar(out=cvec, in0=cvec, scalar1=-8.0, scalar2=None, op0=ALU.mult), reads=["cvec"], writes=["cvec"])

    wr_i = [0]

    def wslot():
        i = wr_i[0] % 4
        wr_i[0] += 1
        return i

    def load_layer_small(l):
        S.dma("pool", gw, gbd[l].rearrange("a k i j -> i a k j"), writes=["gw"])
        S.dma("pool", pw, pbd[l].rearrange("k i j -> i k j"), writes=["pw"])
        S.dma("sp", qkgs, qkg[l], writes=["qkgs"])

    def norm_phase(l, which, NT):
        S.dma("sp", grows[:, which, :], grow[l, which], writes=[f"grows{which}"])
        for i in range(NT):
            S.op("dve", lambda e: e.tensor_tensor(out=junk, in0=x[:, i, :], in1=x[:, i, :], op=ALU.mult), reads=[f"x{i}"], writes=["junk"])
            S.op("dve", lambda e: e.tensor_reduce(out=st8[:, 0:1], in_=junk, axis=AX.X, op=ALU.add), reads=["junk"], writes=["st8"])
            S.op("dve", lambda e: e.tensor_scalar(out=st8[:, 1:2], in0=st8[:, 0:1], scalar1=1.0 / D, scalar2=EPS, op0=ALU.mult, op1=ALU.add), reads=["st8"], writes=["st8"])
            S.op("act", lambda e: e.activation(out=st8[:, 3:4], in_=st8[:, 1:2], func=AF.Sqrt), reads=["st8"], writes=["st8"])
            S.op("dve", lambda e: e.reciprocal(out=st8[:, 2:3], in_=st8[:, 3:4]), reads=["st8"], writes=["st8"])
            S.op("dve", lambda e: e.scalar_tensor_tensor(out=hn, in0=x[:, i, :], scalar=st8[:, 2:3], in1=grows[:, which, :], op0=ALU.mult, op1=ALU.mult),
                 reads=["st8", f"x{i}", f"grows{which}"], writes=["hn"])
            pt = PS[i % 2].bitcast(BF16)
            for k in range(8):
                S.op("pe", lambda e, k=k: e.transpose(out=pt[:, k * 128:(k + 1) * 128], in_=hn[:, k * 128:(k + 1) * 128], identity=identb),
                     reads=["hn", "cst"], writes=[f"P{i % 2}"])
            S.op("act", lambda e: e.activation(out=hT[:, :, i * 128:(i + 1) * 128], in_=pt[:, 0:1024].rearrange("p (k t) -> p k t", k=8), func=AF.Copy),
                 reads=[f"P{i % 2}"], writes=["hT"])

    def load_w(slot, src, key_extra=()):
        S.dma("pool", slot, src, writes=list(key_extra))

    def proj_phase(l, seg):
        NT, N, nseq, L = seg["NT"], seg["N"], seg["nseq"], seg["L"]
        xa_v = xa[:, :, 0:nseq * (3 + L)].rearrange("p k (s c) -> p k s c", s=nseq)
        ub_v = ub[:, :, 0:nseq * (15 + L)].rearrange("p k (s c) -> p k s c", s=nseq)
        if seg["prompt"]:
            S.op("dve", lambda e: e.tensor_copy(out=xa[:, :, 0:3], in_=pconv[:, l, :, :]), reads=["pconv"], writes=["xa"])
            S.op("dve", lambda e: e.tensor_copy(out=ub[:, :, 0:15], in_=ppool[:, l, :, :]), reads=["ppool"], writes=["ub"])
        else:
            S.dma("sp", sconv_s, sconv[l].rearrange("(k p) s j -> p k s j", p=128), writes=["sconv_s"])
            S.dma("sp", slru_s, slru[l].rearrange("(k p) s -> p k s", p=128), writes=["slru_s"])
            S.dma("sp", spool_s, spool[l].rearrange("(k p) s j -> p k s j", p=128), writes=["spool_s"])
            S.op("dve", lambda e: e.tensor_copy(out=xa_v[:, :, :, 0:3], in_=sconv_s), reads=["sconv_s"], writes=["xa"])
            S.op("dve", lambda e: e.tensor_copy(out=ub_v[:, :, :, 0:15], in_=spool_s), reads=["spool_s"], writes=["ub"])
        wv = w_in[l].rearrange("(k p) e -> p k e", p=128)
        for g in range(4):
            si = wslot()
            slot = wring[si].rearrange("p (k e) -> p k e", k=8)
            S.dma("pool", slot, wv[:, :, g * 512:(g + 1) * 512], writes=[f"wr{si}", f"wr{si}b"])
            fm = {0: [0, 1, 2, 3], 1: [0, 1, 2, 3], 2: [0, 1], 3: []}[g]
            for c in fm:
                pb = 2 + (c % 2)
                ps = PS[pb]
                for kc in range(8):
                    S.op("pe", lambda e, kc=kc: e.matmul(ps[:, 0:N], lhsT=slot[:, kc, c * 128:(c + 1) * 128], rhs=hT[:, kc, 0:N], start=(kc == 0), stop=(kc == 7)),
                         reads=[f"wr{si}", "hT"], writes=[f"P{pb}"])
                if g == 0:
                    S.op("act", lambda e: e.activation(out=xa_v[:, c, :, 3:3 + L], in_=ps[:, 0:N].rearrange("p (s t) -> p s t", s=nseq), func=AF.Copy),
                         reads=[f"P{pb}"], writes=["xa"])
                elif g == 1:
                    g_, t_ = TS[0][:, 0:N], TS[1][:, 0:N]
                    S.op("act", lambda e: e.activation(out=g_, in_=ps[:, 0:N], func=AF.Copy), reads=[f"P{pb}"], writes=["ts0"])
                    S.op("dve", lambda e: e.tensor_tensor(out=t_, in0=g_, in1=g_, op=ALU.mult), reads=["ts0"], writes=["ts1"])
                    S.op("dve", lambda e: e.tensor_scalar(out=t_, in0=t_, scalar1=0.044715, scalar2=1.0, op0=ALU.mult, op1=ALU.add), reads=["ts1"], writes=["ts1"])
                    S.op("dve", lambda e: e.tensor_tensor(out=t_, in0=t_, in1=g_, op=ALU.mult), reads=["ts1", "ts0"], writes=["ts1"])
                    S.op("act", lambda e: e.activation(out=t_, in_=t_, func=AF.Sigmoid, scale=1.5957691216057308), reads=["ts1"], writes=["ts1"])
                    S.op("dve", lambda e: e.tensor_tensor(out=ga[:, c, 0:N], in0=t_, in1=g_, op=ALU.mult), reads=["ts1", "ts0"], writes=["ga"])
                else:
                    S.op("act", lambda e: e.activation(out=ub_v[:, c, :, 15:15 + L], in_=ps[:, 0:N].rearrange("p (s t) -> p s t", s=nseq), func=AF.Copy),
                         reads=[f"P{pb}"], writes=["ub"])
            if g == 2:
                slot2 = slot
                si2 = si
        slot3, si3 = slot, si
        import os
        KSUB = int(os.environ.get("KSUB", "9"))
        for i in range(NT if KSUB >= 2 else 0):
            pq, pkv = PS[4], PS[5]
            for kc in range(8):
                S.op("pe", lambda e, kc=kc: e.matmul(pq[:, 0:256], lhsT=hT[:, kc, i * 128:(i + 1) * 128], rhs=slot2[:, kc, 256:512], start=(kc == 0), stop=(kc == 7)),
                     reads=[f"wr{si2}", "hT"], writes=["P4"])
            for kc in range(8):
                S.op("pe", lambda e, kc=kc: e.matmul(pkv[:, 0:512], lhsT=hT[:, kc, i * 128:(i + 1) * 128], rhs=slot3[:, kc, 0:512], start=(kc == 0), stop=(kc == 7)),
                     reads=[f"wr{si3}", "hT"], writes=["P5"])
            S.op("act", lambda e: e.activation(out=qkv[:, 0:256], in_=pq[:, 0:256], func=AF.Copy), reads=["P4"], writes=["qkv"])
            S.op("act", lambda e: e.activation(out=qkv[:, 256:768], in_=pkv[:, 0:512], func=AF.Copy), reads=["P5"], writes=["qkv"])
            if KSUB < 3:
                continue
            sq = TS[2][:, 0:512]
            S.op("dve", lambda e: e.tensor_tensor(out=sq, in0=qkv[:, 0:512], in1=qkv[:, 0:512], op=ALU.mult), reads=["qkv"], writes=["ts2"])
            S.op("dve", lambda e: e.tensor_reduce(out=st8[:, 4:12], in_=sq.rearrange("p (h d) -> p h d", h=8), axis=AX.X, op=ALU.add), reads=["ts2"], writes=["st8"])
            S.op("dve", lambda e: e.tensor_scalar(out=st8[:, 4:12], in0=st8[:, 4:12], scalar1=1.0 / 64, scalar2=EPS, op0=ALU.mult, op1=ALU.add), reads=["st8"], writes=["st8"])
            S.op("act", lambda e: e.activation(out=st8[:, 4:12], in_=st8[:, 4:12], func=AF.Sqrt), reads=["st8"], writes=["st8"])
            S.op("dve", lambda e: e.reciprocal(out=st8[:, 4:12], in_=st8[:, 4:12]), reads=["st8"], writes=["st8"])
            S.op("dve", lambda e: e.tensor_tensor(out=qkv[:, 0:512].rearrange("p (h d) -> p h d", h=8), in0=qkv[:, 0:512].rearrange("p (h d) -> p h d", h=8),
                                                   in1=st8[:, 4:12].unsqueeze(2).to_broadcast([128, 8, 64]), op=ALU.mult), reads=["st8", "qkv"], writes=["qkv"])
            S.op("dve", lambda e: e.tensor_tensor(out=qkv[:, 0:512], in0=qkv[:, 0:512], in1=qkgs, op=ALU.mult), reads=["qkv", "qkgs"], writes=["qkv"])
            if KSUB < 4:
                continue
            r0 = seg["row0"] + i * 128
            kdst, vdst = (kp, vp) if seg["prompt"] else (ksm, vsm)
            S.dma("sp", kdst[l, r0:r0 + 128, :], qkv[:, 256:512], reads=["qkv"])
            S.dma("sp", vdst[l, r0:r0 + 128, :], qkv[:, 512:768], reads=["qkv"])
            S.op("act", lambda e: e.activation(out=qkb[:, 0:256], in_=qkv[:, 0:256], func=AF.Copy, scale=0.125), reads=["qkv"], writes=["qkb"])
            S.op("act", lambda e: e.activation(out=qkb[:, 256:512], in_=qkv[:, 256:512], func=AF.Copy), reads=["qkv"], writes=["qkb"])
            S.op("dve", lambda e: e.tensor_copy(out=Vc[:, i, :], in_=qkv[:, 512:768]), reads=["qkv"], writes=["Vc"])
            if KSUB < 5:
                continue
            for j in range(8):
                pj = PS[6 + j // 4]
                S.op("pe", lambda e, j=j, pj=pj: e.matmul(pj[0:64, (j % 4) * 128:(j % 4 + 1) * 128], lhsT=qkb[:, j * 64:(j + 1) * 64], rhs=identb, start=True, stop=True),
                     reads=["qkb", "cst"], writes=[f"P{6 + j // 4}"])
            S.op("act", lambda e: e.activation(out=QT[:, :, i * 128:(i + 1) * 128], in_=PS[6][0:64, :].rearrange("p (h t) -> p h t", h=4), func=AF.Copy),
                 reads=["P6"], writes=["QT"])
            S.op("dve", lambda e: e.tensor_copy(out=KTc[:, :, i * 128:(i + 1) * 128], in_=PS[7][0:64, :].rearrange("p (h t) -> p h t", h=4)),
                 reads=["P7"], writes=["KTc"])
        if seg["prompt"] and KSUB >= 6:
            j = seg["chunk"]
            S.dma("sp", KTd[l, :, :, j * T:(j + 1) * T], KTc, reads=["KTc"], writes=[f"KTd{l}_{j}"])
            S.dma("sp", Vd[l, j * T:(j + 1) * T, :].rearrange("(b s) f -> s b f", s=128), Vc, reads=["Vc"], writes=[f"Vd{l}_{j}"])

    def lru_phase(l, seg):
        NT, N, nseq, L = seg["NT"], seg["N"], seg["nseq"], seg["L"]
        xa_v = xa[:, :, 0:nseq * (3 + L)].rearrange("p k (s c) -> p k s c", s=nseq)

        def v3(t):
            return t[:, 0:N].rearrange("p (s t) -> p s t", s=nseq)
        for k in range(4):
            pc = lambda j: prms[:, l, k * 8 + j:k * 8 + j + 1]
            xc, xcb = TS[0], TS[1].bitcast(BF16)
            S.op("dve", lambda e: e.tensor_scalar(out=v3(xc), in0=xa_v[:, k, :, 0:L], scalar1=pc(0), scalar2=pc(4), op0=ALU.mult, op1=ALU.add),
                 reads=["xa", "prms"], writes=["ts0"])
            for j in range(1, 4):
                S.op("dve", lambda e, j=j: e.scalar_tensor_tensor(out=v3(xc), in0=xa_v[:, k, :, j:j + L], scalar=pc(j), in1=v3(xc), op0=ALU.mult, op1=ALU.add),
                     reads=["xa", "ts0", "prms"], writes=["ts0"])
            S.op("act", lambda e: e.activation(out=xcb[:, 0:N], in_=xc[:, 0:N], func=AF.Copy), reads=["ts0"], writes=["ts1"])
            S.op("pe", lambda e: e.matmul(PS[2][:, 0:N], lhsT=gw[:, 0, k, :], rhs=xcb[:, 0:N], start=True, stop=True), reads=["gw", "ts1"], writes=["P2"])
            S.op("pe", lambda e: e.matmul(PS[3][:, 0:N], lhsT=gw[:, 1, k, :], rhs=xcb[:, 0:N], start=True, stop=True), reads=["gw", "ts1"], writes=["P3"])
            r_, i_, a_, u_, h_ = TS[2], TS[3], TS[4], TS[5], TS[6]
            S.op("act", lambda e: e.activation(out=r_[:, 0:N], in_=PS[2][:, 0:N], func=AF.Sigmoid, bias=pc(5)), reads=["P2", "prms"], writes=["ts2"])
            S.op("act", lambda e: e.activation(out=i_[:, 0:N], in_=PS[3][:, 0:N], func=AF.Sigmoid, bias=pc(6)), reads=["P3", "prms"], writes=["ts3"])
            S.op("act", lambda e: e.activation(out=a_[:, 0:N], in_=r_[:, 0:N], func=AF.Exp, scale=cvec[:, l, k:k + 1]), reads=["ts2", "cvec"], writes=["ts4"])
            S.op("dve", lambda e: e.tensor_tensor(out=u_[:, 0:N], in0=a_[:, 0:N], in1=a_[:, 0:N], op=ALU.mult), reads=["ts4"], writes=["ts5"])
            S.op("dve", lambda e: e.tensor_scalar(out=u_[:, 0:N], in0=u_[:, 0:N], scalar1=-1.0, scalar2=1.0, op0=ALU.mult, op1=ALU.add), reads=["ts5"], writes=["ts5"])
            S.op("act", lambda e: e.activation(out=u_[:, 0:N], in_=u_[:, 0:N], func=AF.Sqrt), reads=["ts5"], writes=["ts5"])
            S.op("dve", lambda e: e.tensor_tensor(out=i_[:, 0:N], in0=i_[:, 0:N], in1=xc[:, 0:N], op=ALU.mult), reads=["ts3", "ts0"], writes=["ts3"])
            S.op("dve", lambda e: e.tensor_tensor(out=u_[:, 0:N], in0=u_[:, 0:N], in1=i_[:, 0:N], op=ALU.mult), reads=["ts5", "ts3"], writes=["ts5"])
            for s in range(nseq):
                init = plru[:, l, k:k + 1] if seg["prompt"] else slru_s[:, k, s:s + 1]
                S.op("dve", lambda e, s=s, init=init: e.tensor_tensor_scan(out=h_[:, s * L:(s + 1) * L], data0=a_[:, s * L:(s + 1) * L], data1=u_[:, s * L:(s + 1) * L],
                                                                           initial=init, op0=ALU.mult, op1=ALU.add),
                     reads=["ts4", "ts5", "plru", "slru_s"], writes=["ts6"])
            if seg["prompt"]:
                S.op("dve", lambda e: e.tensor_copy(out=plru[:, l, k:k + 1], in_=h_[:, L - 1:L]), reads=["ts6"], writes=["plru"])
            else:
                S.op("dve", lambda e: e.tensor_copy(out=hlast[:, k, :], in_=h_[:, L - 1:N:L]), reads=["ts6"], writes=["hlast"])
            S.op("dve", lambda e: e.tensor_tensor(out=mixT[:, k, 0:N], in0=h_[:, 0:N], in1=ga[:, k, 0:N], op=ALU.mult), reads=["ts6", "ga"], writes=["mixT"])
        if seg["prompt"]:
            S.op("dve", lambda e: e.tensor_copy(out=pconv[:, l, :, :], in_=xa[:, :, L:L + 3]), reads=["xa"], writes=["pconv"])
        else:
            for k in range(4):
                S.dma("sp", convs[l, k * 128:(k + 1) * 128], xa_v[:, k, :, L:L + 3], reads=["xa"])
            S.dma("sp", lrus[l].rearrange("(k p) s -> p k s", p=128), hlast, reads=["hlast"])

    def pool_phase(l, seg):
        NT, N, nseq, L = seg["NT"], seg["N"], seg["nseq"], seg["L"]
        W_ = nseq * (15 + L)
        ub_v = ub[:, :, 0:W_].rearrange("p k (s c) -> p k s c", s=nseq)
        for k in range(2):
            pa = TS[7][:, 0:W_].rearrange("p (s c) -> p s c", s=nseq)
            pb = TS[8][:, 0:W_].rearrange("p (s c) -> p s c", s=nseq)
            C = 15 + L
            u_ = ub_v[:, k]
            S.op("dve", lambda e: e.tensor_tensor(out=pa[:, :, 1:C], in0=u_[:, :, 1:C], in1=u_[:, :, 0:C - 1], op=ALU.add), reads=["ub"], writes=["ts7"])
            S.op("dve", lambda e: e.tensor_tensor(out=pb[:, :, 3:C], in0=pa[:, :, 3:C], in1=pa[:, :, 1:C - 2], op=ALU.add), reads=["ts7"], writes=["ts8"])
            if k == 1:
                S.op("dve", lambda e: e.tensor_tensor(out=pa[:, :, 7:C], in0=pb[:, :, 7:C], in1=pb[:, :, 3:C - 4], op=ALU.add), reads=["ts8"], writes=["ts7"])
                S.op("dve", lambda e: e.tensor_tensor(out=pb[:, :, 15:C], in0=pa[:, :, 15:C], in1=pa[:, :, 7:C - 8], op=ALU.add), reads=["ts7"], writes=["ts8"])
            ws = (2, 4) if k == 0 else (8, 16)
            d_ = TS[9].bitcast(BF16)[:, 0:N].rearrange("p (s t) -> p s t", s=nseq)
            for half, (src, w) in enumerate(((pa, ws[0]), (pb, ws[1]))):
                p0, p1 = half * 64, half * 64 + 64
                S.op("dve", lambda e, src=src, w=w, p0=p0, p1=p1: e.scalar_tensor_tensor(out=d_[p0:p1], in0=src[p0:p1, :, 15:C], scalar=1.0 / w, in1=u_[p0:p1, :, 15:C],
                                                                                       op0=ALU.mult, op1=ALU.subtract), reads=["ts7", "ts8", "ub"], writes=["ts9"])
                if seg["prompt"] and seg["chunk"] == 0:
                    tmp = st8[p0:p1, 0:16]
                    S.op("dve", lambda e, src=src, p0=p0, p1=p1, tmp=tmp: e.tensor_tensor(out=tmp, in0=src[p0:p1, 0, 15:31], in1=icnt[p0:p1, k, :], op=ALU.mult),
                         reads=["ts7", "ts8", "icnt"], writes=["st8"])
                    S.op("dve", lambda e, p0=p0, p1=p1, tmp=tmp: e.tensor_tensor(out=d_[p0:p1, 0, 0:16], in0=tmp, in1=u_[p0:p1, 0, 15:31], op=ALU.subtract),
                         reads=["st8", "ub"], writes=["ts9"])
            S.op("pe", lambda e: e.matmul(PS[2][:, 0:N], lhsT=pw[:, k, :], rhs=TS[9].bitcast(BF16)[:, 0:N], start=True, stop=True), reads=["pw", "ts9"], writes=["P2"])
            S.op("act", lambda e: e.activation(out=mixT[:, 4 + k, 0:N], in_=PS[2][:, 0:N], func=AF.Copy, scale=prms[:, l, 32 + k:33 + k]), reads=["P2", "prms"], writes=["mixT"])
        if seg["prompt"]:
            S.op("dve", lambda e: e.tensor_copy(out=ppool[:, l, :, :], in_=ub[:, :, L:L + 15]), reads=["ub"], writes=["ppool"])
        else:
            for k in range(2):
                S.dma("sp", pools[l, k * 128:(k + 1) * 128], ub_v[:, k, :, L:L + 15], reads=["ub"])

    zps = [PS[0], PS[1], PS[2], PS[3]]
    ops_ = [PS[4], PS[5], PS[6], PS[7]]
    zkeys = ["P0", "P1", "P2", "P3"]
    okeys = ["P4", "P5", "P6", "P7"]

    HALVES = ((0, 1), (2, 3))

    def att_unit(KT, KTkeys, V, Vkeys, nk, c0, c1, mask, first, last):
        for hs in HALVES:
            for h in hs:
                S.op("pe", lambda e, h=h: e.matmul(zps[h][0:nk, c0:c1], lhsT=KT[:, h, :], rhs=QT[:, h, c0:c1], start=True, stop=False),
                     reads=KTkeys + ["QT"], writes=[zkeys[h]])
        zv = []
        for a, hs in enumerate(HALVES):
            h0 = hs[0]
            zv.append(zbig[0:nk, h0 * 512:(h0 + 2) * 512].rearrange("p (h t) -> p h t", h=2)[:, :, c0:c1])
        if mask is not None:
            mw = mask.shape[1]
            mb = mask.unsqueeze(1).to_broadcast([nk, 2, mw])
        for a, hs in enumerate(HALVES):
            h0 = hs[0]
            zk = [zkeys[h] for h in hs]
            S.op("act", lambda e: e.activation(out=e_t[0:nk, h0:h0 + 2, c0:c1], in_=zv[a], func=AF.Exp), reads=zk, writes=[f"e{a}"])
            S.op("act", lambda e: e.activation(out=nl_t[0:nk, h0:h0 + 2, c0:c1], in_=e_t[0:nk, h0:h0 + 2, c0:c1], func=AF.Ln, bias=1.0), reads=[f"e{a}"], writes=[f"nl{a}"])
            if mask is not None:
                S.op("dve", lambda e: e.tensor_tensor(out=nl_t[0:nk, h0:h0 + 2, c0:c0 + mw], in0=nl_t[0:nk, h0:h0 + 2, c0:c0 + mw], in1=mb, op=ALU.mult),
                     reads=[f"nl{a}", "cst"], writes=[f"nl{a}"])
        for a, hs in enumerate(HALVES):
            for h in hs:
                S.op("pe", lambda e, h=h: e.matmul(zps[h][0:nk, c0:c1], lhsT=negtri[0:nk, 0:nk], rhs=nl_t[0:nk, h, c0:c1], start=False, stop=first),
                     reads=[f"nl{a}", "cst"], writes=[zkeys[h]])
                if not first:
                    S.op("pe", lambda e, h=h: e.matmul(zps[h][0:nk, c0:c1], lhsT=negones[:, 0:nk], rhs=R_t[:, h, c0:c1], start=False, stop=True),
                         reads=[f"R{a}", "cst"], writes=[zkeys[h]])
        for a, hs in enumerate(HALVES):
            h0 = hs[0]
            zk = [zkeys[h] for h in hs]
            S.op("act", lambda e: e.activation(out=at_t[0:nk, h0:h0 + 2, c0:c1], in_=zv[a], func=AF.Exp), reads=zk, writes=[f"at{a}"])
            if mask is not None:
                S.op("dve", lambda e: e.tensor_tensor(out=at_t[0:nk, h0:h0 + 2, c0:c0 + mw], in0=at_t[0:nk, h0:h0 + 2, c0:c0 + mw], in1=mb, op=ALU.mult),
                     reads=[f"at{a}", "cst"], writes=[f"at{a}"])
        if not last:
            for a, hs in enumerate(HALVES):
                h0 = hs[0]
                S.op("dve", lambda e: e.tensor_tensor(out=R_t[0:nk, h0:h0 + 2, c0:c1], in0=R_t[0:nk, h0:h0 + 2, c0:c1], in1=nl_t[0:nk, h0:h0 + 2, c0:c1], op=ALU.add),
                     reads=[f"nl{a}", f"R{a}"], writes=[f"R{a}"])
        for a, hs in enumerate(HALVES):
            for h in hs:
                if h % 2 == 0:
                    S.op("pe", lambda e, h=h: e.matmul(ops_[h][0:64, c0:c1], lhsT=V[0:nk, h * 64:(h + 1) * 64], rhs=at_t[0:nk, h, c0:c1], start=False, stop=last),
                         reads=Vkeys + [f"at{a}"], writes=[okeys[h]])
                else:
                    S.op("pe", lambda e, h=h: e.matmul(ops_[h][0:128, c0:c1], lhsT=V[0:nk, (h - 1) * 64:(h + 1) * 64], rhs=at_t[0:nk, h, c0:c1], start=False, stop=last),
                         reads=Vkeys + [f"at{a}"], writes=[okeys[h]])

    def att_begin(c0, c1):
        S.op("dve", lambda e: e.memset(R_t[:, :, c0:c1], 0.0), writes=["R0", "R1"])
        for h in range(4):
            S.op("pe", lambda e, h=h: e.matmul(ops_[h][:, c0:c1], lhsT=zerosb, rhs=hT[:, 0, c0:c1], start=True, stop=False), reads=["cst", "hT"], writes=[okeys[h]])

    def att_end(c0, c1):
        for h in range(4):
            p0 = (h % 2) * 64
            S.op("act", lambda e, h=h, p0=p0: e.activation(out=mixT[p0:p0 + 64, 6 + h // 2, c0:c1], in_=ops_[h][p0:p0 + 64, c0:c1], func=AF.Copy),
                 reads=[okeys[h]], writes=["mixT"])

    def attn_prompt(l, j):
        att_begin(0, T)
        nhist = 4 * j
        units = []
        for jj in (3, 2, 1, 0):
            units.append(("cur", jj))
        npieces = (nhist + 7) // 8
        for p in range(npieces - 1, -1, -1):
            nb = min(8, nhist - 8 * p)
            for b in range(nb - 1, -1, -1):
                units.append(("hist", p, b, nb))
        loaded = {}
        pslot = [0]
        for ui, u in enumerate(units):
            first, last = ui == 0, ui == len(units) - 1
            if u[0] == "cur":
                jj = u[1]
                att_unit(KTc[:, :, jj * 128:(jj + 1) * 128], ["KTc"], Vc[:, jj, :], ["Vc"], 128, jj * 128, T, tri128, first, last)
            else:
                _, p, b, nb = u
                if p not in loaded:
                    s = pslot[0] % 2
                    pslot[0] += 1
                    Kv = kK[:, s * 4096:(s + 1) * 4096].rearrange("p (h t) -> p h t", h=4)
                    Vv = kV[:, s * 2048:(s + 1) * 2048].rearrange("p (b f) -> p b f", b=8)
                    t0 = p * 1024
                    rk = [f"KTd{l}_{c}" for c in range(2 * p, min(2 * p + 2, j))]
                    rv = [f"Vd{l}_{c}" for c in range(2 * p, min(2 * p + 2, j))]
                    S.dma("sp", Kv[:, :, 0:nb * 128], KTd[l, :, :, t0:t0 + nb * 128], reads=rk, writes=[f"kvK{s}"])
                    S.dma("sp", Vv[:, 0:nb, :], Vd[l, t0:t0 + nb * 128, :].rearrange("(b s) f -> s b f", s=128), reads=rv, writes=[f"kvV{s}"])
                    loaded[p] = (Kv, Vv, s)
                Kv, Vv, s = loaded[p]
                att_unit(Kv[:, :, b * 128:(b + 1) * 128], [f"kvK{s}"], Vv[:, b, :], [f"kvV{s}"], 128, 0, T, None, first, last)
        att_end(0, T)

    def attn_sample(l):
        Kv = kK.rearrange("p (h t) -> p h t", h=4)
        Vv = kV.rearrange("p (b f) -> p b f", b=16)
        for s in range(NSQ):
            c0, c1 = s * LS, (s + 1) * LS
            S.dma("pool", Kv[:, :, 0:PAST], ckT[l, s], writes=["kvK0", "kvK1"])
            S.dma("pool", Vv[:, 0:NPB, :], cv[l, s].rearrange("(b s) f -> s b f", s=128), writes=["kvV0", "kvV1"])
            att_begin(c0, c1)
            tl = s // 2
            att_unit(KTc[:, :, tl * 128:(tl + 1) * 128], ["KTc"], Vc[:, tl, :], ["Vc"], 128, c0, c1, maskS[s % 2], True, False)
            for b in range(NPB - 1, -1, -1):
                att_unit(Kv[:, :, b * 128:(b + 1) * 128], ["kvK0", "kvK1"], Vv[:, b, :], ["kvV0", "kvV1"], 128, c0, c1, None, False, b == 0)
            att_end(c0, c1)

    def wout_phase(l, seg):
        NT = seg["NT"]
        wv = w_out[l].rearrange("(k p) e -> p k e", p=128)
        for dh in range(2):
            si = wslot()
            slot = wring[si].rearrange("p (k e) -> p k e", k=8)
            S.dma("pool", slot, wv[:, :, dh * 512:(dh + 1) * 512], writes=[f"wr{si}", f"wr{si}b"])
            for i in range(NT):
                pb = i % 2
                for kc in range(8):
                    S.op("pe", lambda e, kc=kc: e.matmul(PS[pb], lhsT=mixT[:, kc, i * 128:(i + 1) * 128], rhs=slot[:, kc, :], start=(kc == 0), stop=(kc == 7)),
                         reads=["mixT", f"wr{si}"], writes=[f"P{pb}"])
                S.op("dve", lambda e: e.tensor_tensor(out=x[:, i, dh * 512:(dh + 1) * 512], in0=x[:, i, dh * 512:(dh + 1) * 512], in1=PS[pb], op=ALU.add),
                     reads=[f"P{pb}", f"x{i}"], writes=[f"x{i}"])

    def ffn_phase(l, seg):
        NT, N = seg["NT"], seg["N"]
        wg = w_gate[l].rearrange("(k p) f -> p k f", p=128)
        wu = w_up[l].rearrange("(k p) f -> p k f", p=128)
        wd = w_down[l].rearrange("(c p) d -> p c d", p=128)
        for fg in range(DFF // 256):
            si = wslot()
            slot = wring[si].rearrange("p (a k f) -> p a k f", a=2, k=8)
            S.dma("pool", slot[:, 0], wg[:, :, fg * 256:(fg + 1) * 256], writes=[f"wr{si}", f"wr{si}b"])
            S.dma("pool", slot[:, 1], wu[:, :, fg * 256:(fg + 1) * 256], writes=[f"wr{si}b"])
            di = fg % 2
            S.dma("pool", wdring[di], wd[:, fg * 2:fg * 2 + 2, :], writes=[f"wd{di}"])
            ff = ffT[fg % 2]
            for fc in range(2):
                pg, pu = PS[2 + fc * 2], PS[3 + fc * 2]
                for kc in range(8):
                    S.op("pe", lambda e, kc=kc: e.matmul(pg[:, 0:N], lhsT=slot[:, 0, kc, fc * 128:(fc + 1) * 128], rhs=hT[:, kc, 0:N], start=(kc == 0), stop=(kc == 7)),
                         reads=[f"wr{si}", "hT"], writes=[f"P{2 + fc * 2}"])
                for kc in range(8):
                    S.op("pe", lambda e, kc=kc: e.matmul(pu[:, 0:N], lhsT=slot[:, 1, kc, fc * 128:(fc + 1) * 128], rhs=hT[:, kc, 0:N], start=(kc == 0), stop=(kc == 7)),
                         reads=[f"wr{si}b", "hT"], writes=[f"P{3 + fc * 2}"])
                sg = TS[fc][:, 0:N]
                S.op("act", lambda e: e.activation(out=sg, in_=pg[:, 0:N], func=AF.Silu), reads=[f"P{2 + fc * 2}"], writes=[f"ts{fc}"])
                S.op("dve", lambda e: e.tensor_tensor(out=ff[:, fc, 0:N], in0=sg, in1=pu[:, 0:N], op=ALU.mult), reads=[f"ts{fc}", f"P{3 + fc * 2}"], writes=[f"ff{fg % 2}"])
            for i in range(NT):
                for dh in range(2):
                    pb = (i * 2 + dh) % 2
                    for fc in range(2):
                        S.op("pe", lambda e, fc=fc: e.matmul(PS[pb], lhsT=ff[:, fc, i * 128:(i + 1) * 128], rhs=wdring[di][:, fc, dh * 512:(dh + 1) * 512], start=(fc == 0), stop=(fc == 1)),
                             reads=[f"ff{fg % 2}", f"wd{di}"], writes=[f"P{pb}"])
                    S.op("dve", lambda e: e.tensor_tensor(out=x[:, i, dh * 512:(dh + 1) * 512], in0=x[:, i, dh * 512:(dh + 1) * 512], in1=PS[pb], op=ALU.add),
                         reads=[f"P{pb}", f"x{i}"], writes=[f"x{i}"])

    def run_segment(seg):
        NT = seg["NT"]
        src = xp if seg["prompt"] else xs
        r0 = seg["row0"]
        for i in range(NT):
            S.dma("sp", x[:, i, :], src[r0 + i * 128:r0 + (i + 1) * 128, :], writes=[f"x{i}"])
        import os
        KS = int(os.environ.get("KSTOP", "9"))
        for l in range(DEPTH):
            load_layer_small(l)
            norm_phase(l, 0, NT)
            if KS >= 2:
                proj_phase(l, seg)
            if KS >= 3:
                lru_phase(l, seg)
            if KS >= 4:
                pool_phase(l, seg)
            if KS >= 5:
                if seg["prompt"]:
                    attn_prompt(l, seg["chunk"])
                else:
                    attn_sample(l)
            if KS >= 6:
                wout_phase(l, seg)
            if KS >= 7:
                norm_phase(l, 1, NT)
                ffn_phase(l, seg)
        dst = yp if seg["prompt"] else ys
        for i in range(NT):
            S.dma("sp", dst[r0 + i * 128:r0 + (i + 1) * 128, :], x[:, i, :], reads=[f"x{i}"])

    for j in range(NCH):
        run_segment(dict(prompt=True, chunk=j, NT=4, N=T, nseq=1, L=T, row0=j * T))
    for l in range(DEPTH):
        S.dma("sp", convp[l].rearrange("(k p) j -> p k j", p=128), pconv[:, l, :, :], reads=["pconv"])
        S.dma("sp", lrup[l].rearrange("(k p) o -> p k o", p=128), plru[:, l, :].unsqueeze(2), reads=["plru"], allow_slow_non_contiguous=True)
        S.dma("sp", poolp[l].rearrange("(k p) j -> p k j", p=128), ppool[:, l, :, :], reads=["ppool"])
    import os
    if os.environ.get("KSAMPLE", "1") == "1":
        run_segment(dict(prompt=False, chunk=0, NT=2, N=NSQ * LS, nseq=NSQ, L=LS, row0=0))
    S.finish()
    return nc, S


def _consts():
    c = np.zeros((128, 768), np.float32)
    j = np.arange(128)[:, None]; s = np.arange(128)[None, :]
    c[:, 0:128] = -(j >= s).astype(np.float32)
    c[:, 128:256] = -1.0
    c[:, 256:384] = np.eye(128, dtype=np.float32)
    c[:, 384:512] = (j < s).astype(np.float32)
    t64 = (np.arange(64)[:, None] < np.arange(64)[None, :]).astype(np.float32)
    c[0:64, 512:576] = t64
    c[64:128, 576:640] = t64
    return c


def _invcnt():
    ic = np.zeros((128, 2, 16), np.float32)
    ws = {(0, 0): 2, (0, 1): 4, (1, 0): 8, (1, 1): 16}
    pos = np.arange(16)
    for (k, half), w in ws.items():
        ic[half * 64:(half + 1) * 64, k, :] = 1.0 / np.minimum(pos + 1, w)
    return ic


def _blockdiag(w):
    Ld, nb = w.shape[0], w.shape[1]
    out = np.zeros((Ld, nb // 2, 128, 128), np.float32)
    for b in range(nb):
        k, h = b // 2, b % 2
        out[:, k, h * 64:(h + 1) * 64, h * 64:(h + 1) * 64] = w[:, b]
    return out


_CACHE = {}


def kernel(x_prompt, x_sample, cache_k, cache_v, state_conv, state_lru, state_pool,
           attn_norm, w_in, conv_w, conv_b, gate_a_w, gate_a_b, gate_x_w, gate_x_b,
           lru_lambda, pool_w, pool_scale, q_norm, k_norm, w_out, ffn_norm,
           w_gate, w_up, w_down):
    cfg = dict(CFG)
    SEQ, DEPTH, PAST = cfg["SEQ"], cfg["DEPTH"], cfg["PAST"]
    f = lambda a: np.ascontiguousarray(np.asarray(a, dtype=np.float32))
    key = (SEQ, DEPTH, PAST)
    if key not in _CACHE:
        _CACHE[key] = build_program(cfg)
    nc, S = _CACHE[key]
    prm = np.zeros((DEPTH, 128, NPRM), np.float32)
    cw = f(conv_w); cb = f(conv_b); gab = f(gate_a_b).reshape(DEPTH, 512); gxb = f(gate_x_b).reshape(DEPTH, 512); lam = f(lru_lambda); psc = f(pool_scale)
    for k in range(4):
        sl = slice(k * 128, (k + 1) * 128)
        for j in range(4):
            prm[:, :, k * 8 + j] = cw[:, j, sl]
        prm[:, :, k * 8 + 4] = cb[:, sl]
        prm[:, :, k * 8 + 5] = gab[:, sl]
        prm[:, :, k * 8 + 6] = gxb[:, sl]
        prm[:, :, k * 8 + 7] = lam[:, sl]
    for k in range(2):
        prm[:, :, 32 + k] = psc[:, k * 128:(k + 1) * 128]
    grow = np.ascontiguousarray(np.broadcast_to(np.stack([f(attn_norm), f(ffn_norm)], 1)[:, :, None, :], (DEPTH, 2, 128, D)))
    qkg = np.concatenate([np.tile(f(q_norm), (1, 4)), np.tile(f(k_norm), (1, 4))], 1)
    qkg = np.ascontiguousarray(np.broadcast_to(qkg[:, None, :], (DEPTH, 128, 512)))
    gbd = np.ascontiguousarray(np.stack([_blockdiag(f(gate_a_w)), _blockdiag(f(gate_x_w))], 1))
    pbd = _blockdiag(f(pool_w))
    common = dict(xp=f(x_prompt)[0], w_in=f(w_in), w_out=f(w_out), w_gate=f(w_gate), w_up=f(w_up), w_down=f(w_down),
                  gbd=gbd, pbd=pbd, prm=prm, grow=grow, qkg=qkg, cstf=_consts(), invcnt=_invcnt())
    ck = f(cache_k); cvv = f(cache_v); sc = f(state_conv); sl_ = f(state_lru); spl = f(state_pool); xsm = f(x_sample)
    in_maps = []
    for c in range(8):
        b0, b1 = c * NSQ, (c + 1) * NSQ
        m = dict(common)
        m["xs"] = np.ascontiguousarray(xsm[b0:b1].reshape(NSQ * LS, D))
        m["ckT"] = np.ascontiguousarray(ck[:, b0:b1].transpose(0, 1, 4, 3, 2))
        m["cv"] = np.ascontiguousarray(cvv[:, b0:b1].reshape(DEPTH, NSQ, PAST, 256))
        m["sconv"] = np.ascontiguousarray(sc[:, b0:b1].transpose(0, 3, 1, 2))
        m["slru"] = np.ascontiguousarray(sl_[:, b0:b1].transpose(0, 2, 1))
        m["spool"] = np.ascontiguousarray(spl[:, b0:b1].transpose(0, 3, 1, 2))
        in_maps.append(m)
    res = run_bass_kernel_spmd(nc, in_maps, core_ids=list(range(8)))
    R = res.results
    r0 = R[0]
    y_prompt = r0["yp"].reshape(1, SEQ, D)
    nkp = r0["kp"].reshape(DEPTH, 1, SEQ, 4, 64); nvp = r0["vp"].reshape(DEPTH, 1, SEQ, 4, 64)
    ncp = r0["convp"].transpose(0, 2, 1).reshape(DEPTH, 1, 3, 512)
    nhp = r0["lrup"].reshape(DEPTH, 1, 512)
    npp = r0["poolp"].transpose(0, 2, 1).reshape(DEPTH, 1, 15, 256)
    y_sample = np.concatenate([R[c]["ys"].reshape(NSQ, LS, D) for c in range(8)], 0)
    nks = np.concatenate([R[c]["ksm"].reshape(DEPTH, NSQ, LS, 4, 64) for c in range(8)], 1)
    nvs = np.concatenate([R[c]["vsm"].reshape(DEPTH, NSQ, LS, 4, 64) for c in range(8)], 1)
    ncs = np.concatenate([R[c]["convs"].transpose(0, 2, 3, 1) for c in range(8)], 1)
    nhs = np.concatenate([R[c]["lrus"].transpose(0, 2, 1) for c in range(8)], 1)
    nps = np.concatenate([R[c]["pools"].transpose(0, 2, 3, 1) for c in range(8)], 1)
    outs = (y_prompt, y_sample, nkp, nvp, ncp, nhp, npp, nks, nvs, ncs, nhs, nps)
    return tuple(np.ascontiguousarray(o, dtype=np.float32) for o in outs)
```
